# Optimizing a Trainium2 kernel written in Bass

```python
import math
import jax, jax.numpy as jnp
from jax import lax
import numpy as np

D_MODEL = 1024
BATCH = 8
SEQ = 2048
DEPTH = 1
DEC_BATCH = 8
DEC_SEQ = 64
PAST_LEN = 4096

CHUNK = 64
Q_BLOCK = 128
H_A = 4
DK_A = 64
DV_A = 2 * DK_A
W_A = H_A * DV_A
H_B = 4
DK_B = 128
DV_B = 128
W_B = H_B * DV_B
QK_A = H_A * 2 * DK_A
QK_B = H_B * DK_B
W_IN = 2 * QK_A + 2 * W_A + 2 * QK_B + 2 * W_B + 2 * D_MODEL
EPS = 1e-6

kernel_name = "hybrid_diffattn_retention_stream_step"


def _split_points():
    sizes = [QK_A, QK_A, W_A, W_A, QK_B, QK_B, W_B, W_B, D_MODEL, D_MODEL]
    pts, acc = [], 0
    for s in sizes[:-1]:
        acc += s
        pts.append(acc)
    return pts


def rmsnorm(x, g=None):
    xf = x.astype(jnp.float32)
    y = xf * lax.rsqrt(jnp.mean(xf * xf, axis=-1, keepdims=True) + EPS)
    if g is not None:
        y = y * g.astype(jnp.float32)
    return y.astype(x.dtype)


def alibi_slopes():
    return 2.0 ** (-8.0 * jnp.arange(1, H_A + 1, dtype=jnp.float32) / H_A)


def retention_log_decay():
    return jnp.log(1.0 - 2.0 ** (-5.0 - jnp.arange(H_B, dtype=jnp.float32)))


def diff_attention(q, k, v, q_pos, k_pos, lam, slopes):
    s = jnp.einsum('bqhcd,bkhcd->bhcqk', q, k).astype(jnp.float32) * (DK_A ** -0.5)
    dist = jnp.abs(q_pos[:, None] - k_pos[None, :]).astype(jnp.float32)
    s = s - slopes[None, :, None, None, None] * dist[None, None, None]
    allowed = (k_pos[None, :] // CHUNK) <= (q_pos[:, None] // CHUNK)
    s = jnp.where(allowed[None, None, None], s, -jnp.inf)
    p = jax.nn.softmax(s, axis=-1)
    a = p[:, :, 0] - lam * p[:, :, 1]
    return jnp.einsum('bhqk,bkhe->bqhe', a.astype(v.dtype), v)


def diff_attn_prompt(q, k, v, lam, slopes):
    B, T = q.shape[0], q.shape[1]
    nb = T // Q_BLOCK
    qb = q.reshape(B, nb, Q_BLOCK, H_A, 2, DK_A).swapaxes(0, 1)
    starts = jnp.arange(nb, dtype=jnp.int32) * Q_BLOCK
    k_pos = jnp.arange(T, dtype=jnp.int32)

    def block(args):
        qi, st = args
        return diff_attention(qi, k, v, st + jnp.arange(Q_BLOCK, dtype=jnp.int32), k_pos, lam, slopes)

    o = lax.map(block, (qb, starts))
    return o.swapaxes(0, 1).reshape(B, T, H_A, DV_A)


def diff_attn_step(q, k_all, v_all, past, lam, slopes):
    n = q.shape[1]
    q_pos = past + jnp.arange(n, dtype=jnp.int32)
    k_pos = jnp.arange(k_all.shape[1], dtype=jnp.int32)
    return diff_attention(q, k_all, v_all, q_pos, k_pos, lam, slopes)


def retention_prompt(q, k, v, log_g):
    B, T = q.shape[0], q.shape[1]
    nc = T // CHUNK
    qc = q.reshape(B, nc, CHUNK, H_B, DK_B)
    kc = k.reshape(B, nc, CHUNK, H_B, DK_B)
    vc = v.reshape(B, nc, CHUNK, H_B, DV_B)
    j = jnp.arange(CHUNK, dtype=jnp.float32)
    d_intra = jnp.exp(jnp.abs(j[:, None] - j[None, :])[None] * log_g[:, None, None])
    s = jnp.einsum('bnthd,bnshd->bnhts', qc, kc) * d_intra
    intra = jnp.einsum('bnhts,bnshe->bnthe', s, vc)
    zeta = jnp.exp((CHUNK - 1 - j)[None] * log_g[:, None])
    u = jnp.einsum('bnshd,bnshe,hs->nbhde', kc, vc, zeta)
    decay_c = jnp.exp(CHUNK * log_g)[None, :, None, None]

    def step(r, u_n):
        return decay_c * r + u_n, r

    r0 = jnp.zeros((B, H_B, DK_B, DV_B), jnp.float32)
    r_final, r_before = lax.scan(step, r0, u)
    xi = jnp.exp((j + 1.0)[None] * log_g[:, None])
    cross = jnp.einsum('bnthd,nbhde,ht->bnthe', qc, r_before, xi)
    o = (intra + cross).reshape(B, T, H_B, DV_B)
    return o.astype(q.dtype), r_final


def retention_step(q, k, v, r, log_g):
    n = q.shape[1]
    j = jnp.arange(n, dtype=jnp.float32)
    d = jnp.exp(jnp.abs(j[:, None] - j[None, :])[None] * log_g[:, None, None])
    s = jnp.einsum('bthd,bshd->bhts', q, k) * d
    intra = jnp.einsum('bhts,bshe->bthe', s, v)
    xi = jnp.exp((j + 1.0)[None] * log_g[:, None])
    rf = r.astype(jnp.float32)
    cross = jnp.einsum('bthd,bhde,ht->bthe', q, rf, xi)
    zeta = jnp.exp((n - 1.0 - j)[None] * log_g[:, None])
    r_new = jnp.exp(n * log_g)[None, :, None, None] * rf + jnp.einsum('bshd,bshe,hs->bhde', k, v, zeta)
    return (intra + cross).astype(q.dtype), r_new.astype(r.dtype)


def layer_inputs(x, norm_g, w_in, b_gate, qn_g, kn_g):
    B, T = x.shape[0], x.shape[1]
    h = rmsnorm(x, norm_g) @ w_in
    qa, ka, va, za, qb, kb, vb, zb, ga, gb = jnp.split(h, _split_points(), axis=-1)
    qa = rmsnorm(qa.reshape(B, T, H_A, 2, DK_A), qn_g)
    ka = rmsnorm(ka.reshape(B, T, H_A, 2, DK_A), kn_g)
    va = va.reshape(B, T, H_A, DV_A)
    qb = qb.reshape(B, T, H_B, DK_B)
    kb = kb.reshape(B, T, H_B, DK_B) * (DK_B ** -0.5)
    vb = vb.reshape(B, T, H_B, DV_B)
    ga = jax.nn.sigmoid(ga + b_gate[0])
    gb = jax.nn.sigmoid(gb + b_gate[1])
    return qa, ka, va, za, qb, kb, vb, zb, ga, gb


def layer_output(x, oa, ob, za, zb, ga, gb, subln_g, lam_init, w_oa, w_ob, w_out):
    B, T = x.shape[0], x.shape[1]
    oa = (rmsnorm(oa, subln_g) * (1.0 - lam_init)).reshape(B, T, W_A) * jax.nn.silu(za)
    ob = rmsnorm(ob).reshape(B, T, W_B) * jax.nn.silu(zb)
    m = ga * (oa @ w_oa) + gb * (ob @ w_ob)
    return x + m @ w_out


def setup_inputs(seed: int = 0) -> dict:
    key = jax.random.key(seed)
    ks = jax.random.split(key, 20)
    f32 = jnp.float32
    nrm = lambda k, shape: jax.random.normal(k, shape, f32)
    return {
        "x_prompt": nrm(ks[0], (BATCH, SEQ, D_MODEL)),
        "x_sample": nrm(ks[1], (DEC_BATCH, DEC_SEQ, D_MODEL)),
        "cache_k_diff": nrm(ks[2], (DEPTH, DEC_BATCH, PAST_LEN, H_A, 2, DK_A)),
        "cache_v_diff": nrm(ks[3], (DEPTH, DEC_BATCH, PAST_LEN, H_A, DV_A)),
        "state_ret": nrm(ks[4], (DEPTH, DEC_BATCH, H_B, DK_B, DV_B)),
        "norm_g": 1.0 + 0.02 * nrm(ks[5], (DEPTH, D_MODEL)),
        "w_in": nrm(ks[6], (DEPTH, D_MODEL, W_IN)) * D_MODEL ** -0.5,
        "b_gate": 0.01 * nrm(ks[7], (DEPTH, 2, D_MODEL)),
        "qn_g": 1.0 + 0.02 * nrm(ks[8], (DEPTH, DK_A)),
        "kn_g": 1.0 + 0.02 * nrm(ks[9], (DEPTH, DK_A)),
        "lam_q1": 0.1 * nrm(ks[10], (DEPTH, DK_A)),
        "lam_k1": 0.1 * nrm(ks[11], (DEPTH, DK_A)),
        "lam_q2": 0.1 * nrm(ks[12], (DEPTH, DK_A)),
        "lam_k2": 0.1 * nrm(ks[13], (DEPTH, DK_A)),
        "subln_g": 1.0 + 0.02 * nrm(ks[14], (DEPTH, DV_A)),
        "w_o_diff": nrm(ks[15], (DEPTH, W_A, D_MODEL)) * W_A ** -0.5,
        "w_o_ret": nrm(ks[16], (DEPTH, W_B, D_MODEL)) * W_B ** -0.5,
        "w_out": nrm(ks[17], (DEPTH, D_MODEL, D_MODEL)) * D_MODEL ** -0.5,
    }


def reference(x_prompt, x_sample, cache_k_diff, cache_v_diff, state_ret, norm_g, w_in, b_gate,
              qn_g, kn_g, lam_q1, lam_k1, lam_q2, lam_k2, subln_g, w_o_diff, w_o_ret, w_out):
    slopes = alibi_slopes()
    log_g = retention_log_decay()
    past = cache_k_diff.shape[2]
    hp, hs = x_prompt, x_sample
    kp_l, vp_l, rp_l, ks_l, vs_l, rs_l = [], [], [], [], [], []
    for l in range(DEPTH):
        lam_init = 0.8 - 0.6 * math.exp(-0.3 * l)
        lam = (jnp.exp(jnp.sum(lam_q1[l] * lam_k1[l]).astype(jnp.float32))
               - jnp.exp(jnp.sum(lam_q2[l] * lam_k2[l]).astype(jnp.float32)) + lam_init)
        qa, ka, va, za, qb, kb, vb, zb, ga, gb = layer_inputs(hp, norm_g[l], w_in[l], b_gate[l], qn_g[l], kn_g[l])
        oa = diff_attn_prompt(qa, ka, va, lam, slopes)
        ob, r_p = retention_prompt(qb, kb, vb, log_g)
        hp = layer_output(hp, oa, ob, za, zb, ga, gb, subln_g[l], lam_init, w_o_diff[l], w_o_ret[l], w_out[l])
        kp_l.append(ka)
        vp_l.append(va)
        rp_l.append(r_p)
        qa, ka, va, za, qb, kb, vb, zb, ga, gb = layer_inputs(hs, norm_g[l], w_in[l], b_gate[l], qn_g[l], kn_g[l])
        k_all = jnp.concatenate([cache_k_diff[l].astype(ka.dtype), ka], axis=1)
        v_all = jnp.concatenate([cache_v_diff[l].astype(va.dtype), va], axis=1)
        oa = diff_attn_step(qa, k_all, v_all, past, lam, slopes)
        ob, r_s = retention_step(qb, kb, vb, state_ret[l], log_g)
        hs = layer_output(hs, oa, ob, za, zb, ga, gb, subln_g[l], lam_init, w_o_diff[l], w_o_ret[l], w_out[l])
        ks_l.append(ka)
        vs_l.append(va)
        rs_l.append(r_s)
    k_prompt = jnp.stack(kp_l)
    v_prompt = jnp.stack(vp_l)
    ret_prompt = jnp.stack(rp_l)
    k_sample = jnp.stack(ks_l)
    v_sample = jnp.stack(vs_l)
    ret_sample = jnp.stack(rs_l)
    return (hp, hs, k_prompt, v_prompt, ret_prompt, k_sample, v_sample, ret_sample)
```

```python
import math
import numpy as np
import concourse.bass as bass
import concourse.mybir as mybir
from concourse.bass_utils import run_bass_kernel_spmd

F32 = mybir.dt.float32
BF = mybir.dt.bfloat16
AF = mybir.ActivationFunctionType
ALU = mybir.AluOpType
AX = mybir.AxisListType

D = 1024
SEQ = 2048
NS = 64
PAST = 4096
NTOK = NS + SEQ
EPS = 1e-6
LAM_INIT = 0.8 - 0.6 * math.exp(0.0)
SLOPES = [2.0 ** (-8.0 * h / 4) for h in range(1, 5)]
LOGG = [math.log(1.0 - 2.0 ** (-5.0 - h)) for h in range(4)]
DECAY = [math.exp(64 * g) for g in LOGG]
NEG = -30000.0

ENGS = ("pe", "act", "dve", "pool", "sp")


class _Rec:
    def __init__(self):
        self.call = None

    def __getattr__(self, name):
        def f(*a, **k):
            self.call = (name, a, k)
            return None
        return f


class Prog:
    def __init__(self, nc, n_dma_sems=24):
        self.nc = nc
        self.ops = []
        self.last_writer = {}
        self.readers = {}
        self.pending_bar = {}
        self.n_dma_sems = n_dma_sems
        self.last_on_eng = {}
        self.dma_since_bar = []

    def op(self, eng, fn, reads=(), writes=(), dma=False, **kw):
        if isinstance(fn, str):
            name = fn
            fn = (lambda e, name=name, kw=kw: getattr(e, name)(**kw))
        else:
            rec = _Rec()
            fn(rec)
            rname, rargs, rkw = rec.call
            fn = (lambda e, rname=rname, rargs=rargs, rkw=rkw: getattr(e, rname)(*rargs, **rkw))
        idx = len(self.ops)
        deps = set()
        for k in reads:
            w = self.last_writer.get(k)
            if w is not None:
                deps.add((w, "raw"))
            if k.startswith("ps"):
                for e2, r in self.readers.get(k, {}).items():
                    if e2 != eng and not isinstance(r, list):
                        deps.add((r, "raw"))
        for k in writes:
            w = self.last_writer.get(k)
            if w is not None:
                deps.add((w, "waw"))
            for r in self.readers.get(k, {}).values():
                if isinstance(r, list):
                    for rr in r:
                        deps.add((rr, "war"))
                else:
                    deps.add((r, "war"))
        for k in writes:
            self.last_writer[k] = idx
            self.readers[k] = {}
        for k in reads:
            rd = self.readers.setdefault(k, {})
            if dma:
                rd.setdefault("dma", []).append(idx)
            else:
                rd[eng] = idx
        if eng in self.pending_bar:
            for d in self.pending_bar.pop(eng):
                deps.add((d, "raw"))
        self.ops.append(dict(eng=eng, fn=fn, deps=deps, dma=dma, idx=idx))
        if dma:
            self.dma_since_bar.append(idx)
        else:
            self.last_on_eng[eng] = idx
        return idx

    def barrier(self):
        frontier = set(self.last_on_eng.values()) | set(self.dma_since_bar)
        self.dma_since_bar = []
        self.last_writer = {}
        self.readers = {}
        for e in ENGS:
            cur = self.pending_bar.get(e, set())
            self.pending_bar[e] = cur | frontier

    def emit(self):
        nc = self.nc
        ops = self.ops
        pos = {}
        cnt = {}
        for o in ops:
            if not o["dma"]:
                c = cnt.get(o["eng"], 0)
                pos[o["idx"]] = c
                cnt[o["eng"]] = c + 1
        for o in ops:
            real = set()
            for (d, kind) in o["deps"]:
                p = ops[d]
                if d == o["idx"]:
                    continue
                if not o["dma"] and not p["dma"] and p["eng"] == o["eng"]:
                    if o["eng"] == "pe":
                        continue
                    if pos[o["idx"]] - pos[d] > 6:
                        continue
                real.add(d)
            o["rdeps"] = real
        needs = [False] * len(ops)
        for o in ops:
            for d in o["rdeps"]:
                needs[d] = True
        sems = {}
        nd = self.n_dma_sems // 2
        names = list(ENGS) + ["dmaS%d" % i for i in range(nd)] + ["dmaP%d" % i for i in range(nd)]
        cms = []
        for n in names:
            cm = nc.semaphore("s_" + n)
            sems[n] = cm.__enter__()
            cms.append(cm)
        count = {n: 0 for n in names}
        dma_i = {"sp": 0, "pool": 0}
        for o in ops:
            if o["dma"]:
                q = o["eng"]
                n = ("dmaS%d" if q == "sp" else "dmaP%d") % (dma_i[q] % nd)
                dma_i[q] += 1
                o["prev_same_sem"] = (n, count[n])
                count[n] += 16
                o["sig"] = (n, count[n])
            elif needs[o["idx"]]:
                n = o["eng"]
                count[n] += 1
                o["sig"] = (n, count[n])
            else:
                o["sig"] = None
        final_counts = dict(count)
        by_eng = {e: [o for o in ops if o["eng"] == e] for e in ENGS}

        def run_engine(ename, eng, final=False):
            waited = {}

            def wait(n, v):
                if v <= 0 or waited.get(n, 0) >= v:
                    return
                eng.wait_ge(sems[n], v)
                waited[n] = v

            for o in by_eng[ename]:
                need = {}
                for d in o["rdeps"]:
                    n, v = ops[d]["sig"]
                    if need.get(n, 0) < v:
                        need[n] = v
                if o["dma"]:
                    n, v = o["prev_same_sem"]
                    if need.get(n, 0) < v:
                        need[n] = v
                for n, v in need.items():
                    wait(n, v)
                ins = o["fn"](eng)
                if o["sig"] is not None:
                    n, v = o["sig"]
                    ins.then_inc(sems[n], 16 if o["dma"] else 1)
            if final:
                for n in names:
                    if n.startswith("dma"):
                        wait(n, final_counts[n])

        with nc.Block() as block:
            @block.tensor
            def _(e):
                run_engine("pe", e)

            @block.scalar
            def _(e):
                run_engine("act", e)

            @block.vector
            def _(e):
                run_engine("dve", e)

            @block.gpsimd
            def _(e):
                run_engine("pool", e)

            @block.sync
            def _(e):
                run_engine("sp", e, final=True)
        for cm in reversed(cms):
            cm.__exit__(None, None, None)
        self.stats = {e: len(by_eng[e]) for e in ENGS}
        self.stats["signals"] = {k: v for k, v in final_counts.items() if not k.startswith("dma")}


def _bias_cols():
    cols = []
    index = {}
    p = np.arange(128, dtype=np.float64)

    def add(key, vec):
        index[key] = len(cols)
        cols.append(np.asarray(vec, dtype=np.float64) * np.ones(128))

    for h in range(4):
        s = SLOPES[h]
        for m in range(1, 13):
            add(("pp", h, m), s * (p - 128.0 * m - 256.0))
        for m in range(4):
            add(("pd", h, m), s * (p + 128.0 * m - 256.0))
        for j in range(32):
            add(("sp", h, j), s * (128.0 * j + p - (PAST + 32.0)))
        add(("sd", h), s * (p - 32.0))
    return np.stack(cols, axis=1).astype(np.float32), index


def _const_tables():
    p = np.arange(128)[:, None]
    jj = np.arange(512)[None, :]
    T = np.zeros((128, 4, 512), np.float32)
    for h in range(4):
        s = SLOPES[h]
        val = s * (jj - np.abs(jj - p) - p)
        ok = (p // 64) <= (jj // 64)
        T[:, h, :] = np.where(ok, val, NEG)
    Dm = np.zeros((128, 4, 128), np.float32)
    t = np.arange(128)[None, :]
    sidx = np.arange(128)[:, None]
    for h in range(4):
        d = np.exp(np.abs(t - sidx) * LOGG[h])
        Dm[:, h, :] = np.where((t // 64) == (sidx // 64), d, 0.0)
    zeta = np.zeros((128, 4), np.float32)
    xi = np.zeros((128, 4, 64), np.float32)
    for h in range(4):
        zeta[:, h] = np.exp((63 - (np.arange(128) % 64)) * LOGG[h])
        xi[:, h, :] = np.exp((np.arange(64) + 1.0) * LOGG[h])[None, :]
    return T, Dm, zeta, xi


class _Stop(Exception):
    pass


def build_program(stop_at=None):
    import os
    stop_at = stop_at if stop_at is not None else os.environ.get("KSTOP")
    nc_holder = {}
    try:
        return _build_program(stop_at, nc_holder)
    except _Stop:
        nc, P, bt = nc_holder["v"]
        P.emit()
        return nc, P, bt


def _build_program(stop_at, nc_holder):
    nc = bass.Bass("TRN2", target_bir_lowering=False)
    P = Prog(nc)
    bias_tab_np, BI = _bias_cols()
    NBC = bias_tab_np.shape[1]
    nc_holder["v"] = (nc, P, bias_tab_np)

    def ckpt(name):
        if stop_at is not None and stop_at == name:
            raise _Stop()

    def din(name, shape):
        return nc.dram_tensor(name, list(shape), F32, kind="ExternalInput").ap()

    def dout(name, shape):
        return nc.dram_tensor(name, list(shape), F32, kind="ExternalOutput").ap()

    x_p = din("x_p", [SEQ, D])
    x_s = din("x_s", [NS, D])
    ck = din("ck", [PAST, 512])
    cv = din("cv", [PAST, 512])
    st = din("st", [4, 128, 128])
    w_in = din("w_in", [D, 6144])
    w_oa = din("w_oa", [512, D])
    w_ob = din("w_ob", [512, D])
    w_out = din("w_out", [D, D])
    norm_g = din("norm_g", [D])
    bgT_d = din("bgT", [128, 16])
    qk_g = din("qk_g", [128, 2])
    kn_g = din("kn_g", [64])
    lamv = din("lamv", [4, 64])
    subg = din("subg", [128, 1])
    c_ident = din("c_ident", [128, 128])
    c_T = din("c_T", [128, 4 * 512])
    c_D = din("c_D", [128, 4 * 128])
    c_zeta = din("c_zeta", [128, 4])
    c_xi = din("c_xi", [128, 4 * 64])
    c_bias = din("c_bias", [128, NBC])

    y_p = dout("y_p", [SEQ, D])
    y_s = dout("y_s", [NS, D])
    k_p = dout("k_p", [SEQ, 512])
    v_p = dout("v_p", [SEQ, 512])
    r_p = dout("r_p", [4, 128, 128])
    k_s = dout("k_s", [NS, 512])
    v_s = dout("v_s", [NS, 512])
    r_s = dout("r_s", [4, 128, 128])

    def sb(name, shape, dt):
        return nc.alloc_sbuf_tensor(name, list(shape), dt)

    XNT = sb("XNT", [128, 8, NTOK], BF)
    W = sb("W", [128, 8, 2048], BF)
    KV = sb("KV", [128, 16384], BF)
    oaT = sb("oaT", [128, 4, NTOK], BF)
    obT = sb("obT", [128, 4, NTOK], BF)
    ident = sb("ident", [128, 128], BF)
    ones1 = sb("ones1", [128, 128], BF)
    ones128 = sb("ones128", [128, 128], BF)
    onesw = sb("onesw", [128, 4, 4, 128], BF)
    Ttab = sb("Ttab", [128, 4, 512], BF)
    Dtab = sb("Dtab", [128, 4, 128], F32)
    zeta = sb("zeta", [128, 4], F32)
    xi = sb("xi", [128, 4, 64], F32)
    btab = sb("btab", [128, NBC], F32)
    bgh = sb("bgh", [128, 16], F32)
    qkg = sb("qkg", [128, 2], F32)
    gkbc8 = sb("gkbc8", [128, 8, 64], F32)
    gsub = sb("gsub", [128, 1], F32)
    neglam = sb("neglam", [128, 1], F32)
    nhalf = sb("nhalf", [128, 8], F32)
    epscol = sb("epscol", [128, 3], F32)
    lamt = sb("lamt", [128, 4, 64], F32)
    lams = sb("lams", [128, 4], F32)
    r32 = sb("r32", [128, 4, 128], F32)
    ssx = sb("ssx", [128, 17], F32)
    rsx = sb("rsx", [128, 17], F32)
    kT_s = sb("kT_s", [128, 4, NS], BF)
    v_sbf = sb("v_sbf", [128, 512], BF)
    TMP = sb("TMP", [128, 24576], BF)

    kT = KV[:, 0:8192].rearrange("p (h t) -> p h t", h=4)
    v_bf = KV[:, 8192:16384].rearrange("p (b c) -> p b c", c=512)
    kc_tm = KV[:, 0:4096].rearrange("p (b c) -> p b c", c=128)
    kcT = KV[:, 4096:8192]
    cvh = KV[:, 8192:12288].rearrange("p (b c) -> p b c", c=128)
    cvh2t = sb("cvh2", [128, 4096], BF)
    cv_flat = [KV[:, 8192:12288], cvh2t[:, :]]
    cvhs = [cv_flat[0].rearrange("p (b c) -> p b c", c=128), cv_flat[1].rearrange("p (b c) -> p b c", c=128)]
    kc_tms = [KV[:, 0:4096].rearrange("p (b c) -> p b c", c=128),
              KV[:, 12288:16384].rearrange("p (b c) -> p b c", c=128)]

    def load_cache_head(hh):
        dma("pool", kc_tms[hh % 2], ck.rearrange("(b p) c -> p b c", p=128)[:, :, hh * 128:(hh + 1) * 128],
            writes=["kc_tm%d" % (hh % 2)])
        dma("pool", cvhs[hh % 2], cv.rearrange("(b p) c -> p b c", p=128)[:, :, hh * 128:(hh + 1) * 128],
            writes=["cvh%d" % (hh % 2)])
    WO = KV
    Woa = KV[:, 0:4096].rearrange("p (k n) -> p k n", k=4)
    Wob = KV[:, 4096:8192].rearrange("p (k n) -> p k n", k=4)
    Wout = KV[:, 8192:16384].rearrange("p (k n) -> p k n", k=8)

    psall = nc.alloc_psum_tensor("psall", [128, 4096], F32)
    banks = [psall[:, i * 512:(i + 1) * 512] for i in range(8)]
    grot = [0]

    def gbank():
        b = grot[0] % 4
        grot[0] += 1
        return b

    hrot = [0]

    def hbank():
        b = 4 + hrot[0] % 4
        hrot[0] += 1
        return b

    def pk(b):
        return "ps%d" % b

    tmp_off = [0]

    def talloc(name, ncols, dt):
        n = ncols * (2 if dt == F32 else 1)
        o = tmp_off[0]
        assert o + n <= 24576, ("TMP overflow", name)
        tmp_off[0] = o + n
        ap = TMP[:, o:o + n]
        if dt == F32:
            ap = ap.bitcast(F32)
        return ap

    def treset():
        tmp_off[0] = 0

    def dma(eng, out, in_, reads=(), writes=()):
        P.op(eng, lambda e: e.dma_start(out=out, in_=in_), reads=reads, writes=writes, dma=True)

    dma("pool", ident[:], c_ident, writes=["ident"])
    dma("pool", Ttab[:].rearrange("p h c -> p (h c)"), c_T, writes=["Ttab"])
    dma("sp", Dtab[:].rearrange("p h c -> p (h c)"), c_D, writes=["Dtab"])
    dma("sp", zeta[:], c_zeta, writes=["zeta"])
    dma("sp", xi[:].rearrange("p h c -> p (h c)"), c_xi, writes=["xi"])
    dma("sp", btab[:], c_bias, writes=["btab"])
    dma("sp", bgh[:], bgT_d, writes=["bgh"])
    dma("sp", qkg[:], qk_g, writes=["qkg"])
    dma("sp", gsub[:], subg, writes=["gsub"])
    dma("sp", gkbc8[:, 0, :], kn_g.partition_broadcast(128), writes=["gkbc8"])
    for i in range(4):
        dma("sp", lamt[:, i, :], lamv[i].partition_broadcast(128), writes=["lamt"])
    P.op("dve", lambda e: e.memset(ones1[:], 1.0), writes=["ones1"])
    P.op("dve", lambda e: e.memset(ones128[:], 1.0 / 128), writes=["ones128"])
    for hh in range(4):
        for dd in range(4):
            P.op("dve", "memset", writes=["onesw"], ap=onesw[:, hh, dd, :],
                 constant=float(math.exp(-SLOPES[hh] * 128.0 * dd)))
    P.op("dve", lambda e: e.memset(nhalf[:], -0.5), writes=["nhalf"])
    P.op("dve", lambda e: e.memset(epscol[:, 0:1], EPS), writes=["epscol0"])
    P.op("dve", lambda e: e.memset(epscol[:, 1:2], 64 * EPS), writes=["epscol1"])
    P.op("dve", lambda e: e.memset(epscol[:, 2:3], 1024 * EPS), reads=["epscol0", "epscol1"], writes=["epscol"])
    P.op("dve", lambda e: e.memset(ssx[:], 0.0), writes=["ssx%d" % b for b in range(17)])
    P.op("dve", lambda e: e.tensor_scalar(out=bgh[:], in0=bgh[:], scalar1=0.5, scalar2=None, op0=ALU.mult),
         reads=["bgh"], writes=["bgh"])
    P.op("dve", lambda e: e.tensor_scalar(out=qkg[:, 1:2], in0=qkg[:, 1:2], scalar1=8.0, scalar2=None, op0=ALU.mult),
         reads=["qkg"], writes=["qkg"])
    P.op("dve", lambda e: e.tensor_scalar(out=gsub[:], in0=gsub[:], scalar1=(1.0 - LAM_INIT) * 0.5, scalar2=None,
                                          op0=ALU.mult), reads=["gsub"], writes=["gsub"])
    for j in range(1, 8):
        P.op("dve", lambda e, j=j: e.tensor_copy(out=gkbc8[:, j, :], in_=gkbc8[:, 0, :]), reads=["gkbc8"],
             writes=["gkbc8_%d" % j])
    P.op("dve", lambda e: e.tensor_scalar(out=gkbc8[:].rearrange("p a b -> p (a b)"),
                                          in0=gkbc8[:].rearrange("p a b -> p (a b)"), scalar1=8.0, scalar2=None,
                                          op0=ALU.mult), reads=["gkbc8"] + ["gkbc8_%d" % j for j in range(1, 8)],
         writes=["gkbc8"])
    P.op("dve", lambda e: e.tensor_tensor(out=lamt[:, 0, :], in0=lamt[:, 0, :], in1=lamt[:, 1, :], op=ALU.mult),
         reads=["lamt"], writes=["lamt"])
    P.op("dve", lambda e: e.tensor_tensor(out=lamt[:, 2, :], in0=lamt[:, 2, :], in1=lamt[:, 3, :], op=ALU.mult),
         reads=["lamt"], writes=["lamt"])
    P.op("dve", lambda e: e.tensor_reduce(out=lams[:, 0:1], in_=lamt[:, 0, :], axis=AX.X, op=ALU.add),
         reads=["lamt"], writes=["lams0"])
    P.op("dve", lambda e: e.tensor_reduce(out=lams[:, 1:2], in_=lamt[:, 2, :], axis=AX.X, op=ALU.add),
         reads=["lamt"], writes=["lams1"])
    P.op("act", lambda e: e.activation(out=lams[:, 2:4], in_=lams[:, 0:2], func=AF.Exp), reads=["lams0", "lams1"],
         writes=["lams2"])
    P.op("dve", lambda e: e.tensor_tensor(out=neglam[:], in0=lams[:, 3:4], in1=lams[:, 2:3], op=ALU.subtract),
         reads=["lams2"], writes=["neglam"])
    P.op("dve", lambda e: e.tensor_scalar(out=neglam[:], in0=neglam[:], scalar1=-LAM_INIT, scalar2=None, op0=ALU.add),
         reads=["neglam"], writes=["neglam"])

    def blk_info(b):
        if b == 0:
            return 0, NS
        return NS + (b - 1) * 128, 128

    def load_w(col0):
        for g in range(4):
            dma("pool", W[:, :, g * 512:(g + 1) * 512],
                w_in.rearrange("(kc p) n -> p kc n", p=128)[:, :, col0 + g * 512:col0 + (g + 1) * 512],
                writes=["W%d" % g])

    load_w(0)
    treset()
    gbc = talloc("gbc", 1024, F32)
    xin = [talloc("xin%d" % i, 1024, F32) for i in range(4)]
    junk = talloc("junk", 1024, BF)
    xnb = [talloc("xnb%d" % i, 1024, BF) for i in range(4)]
    dma("sp", gbc, norm_g.partition_broadcast(128), writes=["gbc"])
    P.op("dve", lambda e: e.tensor_scalar(out=gbc, in0=gbc, scalar1=32.0, scalar2=None, op0=ALU.mult),
         reads=["gbc"], writes=["gbc"])
    p0_pending = []
    for b in range(17):
        c0, nt = blk_info(b)
        xi_ = xin[b % 4]
        xb = xnb[b % 4]
        src = x_s if b == 0 else x_p[(b - 1) * 128:b * 128, :]
        dma("sp", xi_[0:nt, :], src, writes=["xin%d" % (b % 4)])
        P.op("act", lambda e, xi_=xi_, nt=nt, b=b: e.activation(out=junk[0:nt, :], in_=xi_[0:nt, :], func=AF.Square,
                                                               accum_out=ssx[0:nt, b:b + 1]),
             reads=["xin%d" % (b % 4)], writes=["junk", "ssx%d" % b])
        P.op("act", lambda e, nt=nt, b=b: e.activation(out=rsx[0:nt, b:b + 1], in_=ssx[0:nt, b:b + 1], func=AF.Ln,
                                                      bias=epscol[0:nt, 2:3], scale=1.0),
             reads=["ssx%d" % b, "epscol"], writes=["rsxa%d" % b])
        P.op("act", lambda e, nt=nt, b=b: e.activation(out=rsx[0:nt, b:b + 1], in_=rsx[0:nt, b:b + 1], func=AF.Exp,
                                                      scale=-0.5), reads=["rsxa%d" % b], writes=["rsx%d" % b])
        P.op("dve", lambda e, xi_=xi_, xb=xb, nt=nt, b=b: e.scalar_tensor_tensor(
            out=xb[0:nt, :], in0=xi_[0:nt, :], scalar=rsx[0:nt, b:b + 1], in1=gbc[0:nt, :], op0=ALU.mult,
            op1=ALU.mult), reads=["xin%d" % (b % 4), "rsx%d" % b, "gbc"], writes=["xnb%d" % (b % 4)])
        while p0_pending:
            P.op("dve", "tensor_copy", **p0_pending.pop(0))
        bk = gbank()
        pst = banks[bk][:, 0:512].bitcast(BF)
        for kc in range(8):
            P.op("pe", lambda e, pst=pst, xb=xb, nt=nt, kc=kc: e.transpose(
                out=pst[:, kc * 128:kc * 128 + nt], in_=xb[0:nt, kc * 128:(kc + 1) * 128], identity=ident[0:nt, 0:nt]),
                reads=["xnb%d" % (b % 4), "ident"], writes=[pk(bk)])
        p0_pending.append(dict(reads=[pk(bk)], writes=["XNT%d" % b], out=XNT[:, :, c0:c0 + nt],
                               in_=pst.rearrange("p (k t) -> p k t", k=8)[:, :, 0:nt]))
    while p0_pending:
        P.op("dve", "tensor_copy", **p0_pending.pop(0))
    XNT_ALL = ["XNT%d" % b for b in range(17)]
    ckpt("p0")

    def tile_info(t):
        if t == 0:
            return 0, NS, [0]
        return NS + (t - 1) * 512, 512, [1 + 4 * (t - 1) + i for i in range(4)]

    def mm_tm(bk, c0, nt, wc0, wkey):
        for kc in range(8):
            P.op("pe", lambda e, kc=kc: e.matmul(banks[bk][0:nt, :], lhsT=XNT[:, kc, c0:c0 + nt],
                                                  rhs=W[:, kc, wc0:wc0 + 512], start=(kc == 0), stop=(kc == 7)),
                 reads=XNT_ALL + [wkey], writes=[pk(bk)])

    def mm_fm(bk, c0, nq, wc0, wkey):
        for kc in range(8):
            P.op("pe", lambda e, kc=kc: e.matmul(banks[bk][:, 0:nq], lhsT=W[:, kc, wc0:wc0 + 128],
                                                  rhs=XNT[:, kc, c0:c0 + nq], start=(kc == 0), stop=(kc == 7)),
                 reads=XNT_ALL + [wkey], writes=[pk(bk)])

    P.barrier()
    treset()
    load_cache_head(0)
    qT_t = talloc("qT_t", 2048, BF).rearrange("p (h t) -> p h t", h=4)
    szaT = talloc("szaT", 2048, BF).rearrange("p (h t) -> p h t", h=4)
    sq = [talloc("sq%d" % i, 512, F32) for i in range(4)]
    ss8 = [talloc("ss8%d" % i, 8, F32) for i in range(4)]
    rs8 = [talloc("rs8%d" % i, 8, F32) for i in range(4)]
    qnb = [talloc("qnb%d" % i, 512, BF) for i in range(2)]
    knf = [talloc("knf0", 512, F32)] * 2
    kob = [talloc("kob%d" % i, 512, F32) for i in range(2)]
    vfb = [talloc("vfb%d" % i, 512, F32) for i in range(2)]
    thb = [talloc("thb0", 512, F32)] * 2
    ptile2 = [talloc("pt%d" % i, 1024, BF) for i in range(3)]
    dgb = [talloc("dg0", 512, F32)] * 2
    fzz = talloc("fzz", 1024, F32)
    fz0 = fzz[:, 0:512]
    fz1 = fzz[:, 512:1024]
    fto = talloc("fto", 1024, F32)
    ft0 = fto[:, 0:512]
    fo = fto[:, 512:1024]
    fms = fz0
    frs = ft0
    fosq = talloc("fosq", 512, BF)
    nctr = [0]

    def qk_norm(bk, nt, is_k, b, c0loc, tcol0):
        i = nctr[0] % 2
        nctr[0] += 1
        s_, s8, r8 = sq[i], ss8[i], rs8[i]
        P.op("act", lambda e: e.activation(out=s_[0:nt, :], in_=banks[bk][0:nt, :], func=AF.Square),
             reads=[pk(bk)], writes=["sq%d" % i])
        P.op("dve", lambda e: e.tensor_reduce(out=s8[0:nt, :], in_=s_[0:nt, :].rearrange("p (g d) -> p g d", g=8),
                                              axis=AX.X, op=ALU.add), reads=["sq%d" % i], writes=["ss8%d" % i])
        P.op("act", lambda e: e.activation(out=r8[0:nt, :], in_=s8[0:nt, :], func=AF.Ln, bias=epscol[0:nt, 1:2],
                                           scale=1.0), reads=["ss8%d" % i, "epscol"], writes=["rs8a%d" % i])
        P.op("act", lambda e: e.activation(out=r8[0:nt, :], in_=r8[0:nt, :], func=AF.Exp, scale=-0.5),
             reads=["rs8a%d" % i], writes=["rs8%d" % i])
        rb = r8[0:nt, :].unsqueeze(2).to_broadcast([nt, 8, 64])
        src3 = banks[bk][0:nt, :].rearrange("p (g d) -> p g d", g=8)
        if not is_k:
            qn = qnb[i]
            P.op("dve", lambda e: e.tensor_tensor(out=qn[0:nt, :].rearrange("p (g d) -> p g d", g=8), in0=src3,
                                                  in1=rb, op=ALU.mult), reads=[pk(bk), "rs8%d" % i],
                 writes=["qnb%d" % i])
            nkey = "qnb%d" % i
            gcol = qkg[:, 0:1]
        else:
            kf, ko, qn = knf[i], kob[i], qnb[i]
            P.op("dve", lambda e: e.tensor_tensor(out=kf[0:nt, :].rearrange("p (g d) -> p g d", g=8), in0=src3,
                                                  in1=rb, op=ALU.mult), reads=[pk(bk), "rs8%d" % i],
                 writes=["knf0"])
            P.op("dve", lambda e: e.tensor_tensor(out=ko[0:nt, :], in0=kf[0:nt, :],
                                                  in1=gkbc8[0:nt].rearrange("p a b -> p (a b)"), op=ALU.mult),
                 reads=["knf0", "gkbc8"], writes=["kob%d" % i])
            dst = k_s if b == 0 else k_p[(b - 1) * 128:b * 128, :]
            dma("sp", dst, ko[0:nt, :], reads=["kob%d" % i])
            P.op("act", lambda e: e.activation(out=qn[0:nt, :], in_=kf[0:nt, :], func=AF.Copy),
                 reads=["knf0"], writes=["qnb%d" % i])
            nkey = "qnb%d" % i
            gcol = qkg[:, 1:2]
        tb = 6 + trot[0] % 2
        trot[0] += 1
        pst = banks[tb][:, 0:256].bitcast(BF)
        for h in range(4):
            P.op("pe", lambda e, h=h: e.transpose(out=pst[:, h * 128:h * 128 + nt],
                                                  in_=qn[0:nt, h * 128:(h + 1) * 128], identity=ident[0:nt, 0:nt]),
                 reads=[nkey, "ident"], writes=[pk(tb)])
        src = pst.rearrange("p (h t) -> p h t", h=4)[:, :, 0:nt]
        if not is_k:
            P.op("dve", lambda e: e.tensor_scalar(out=qT_t[:, :, c0loc:c0loc + nt], in0=src, scalar1=gcol,
                                                  scalar2=None, op0=ALU.mult), reads=[pk(tb), "qkg"],
                 writes=["qT_t"])
        else:
            if b == 0:
                dstT = kT_s[:, :, 0:nt]
                key = "kT_s"
            else:
                dstT = kT[:, :, tcol0:tcol0 + nt]
                key = "kT%d" % b
            P.op("dve", lambda e: e.tensor_scalar(out=dstT, in0=src, scalar1=gcol, scalar2=None, op0=ALU.mult),
                 reads=[pk(tb), "qkg"], writes=[key])

    def silu_fm(bk, nq, dst, dkey, i):
        th = thb[i % 2]
        P.op("act", lambda e: e.activation(out=th[:, 0:nq], in_=banks[bk][:, 0:nq], func=AF.Tanh, scale=0.5),
             reads=[pk(bk)], writes=["thb0"])
        P.op("dve", lambda e: e.scalar_tensor_tensor(out=dst, in0=th[:, 0:nq], scalar=1.0, in1=banks[bk][:, 0:nq],
                                                     op0=ALU.add, op1=ALU.mult),
             reads=["thb0", pk(bk)], writes=[dkey])

    pctr = [0]

    srotA = [0]

    def attention_head(t, h, nq, kblocks, pending_fin=None):
        O = [4, 5]
        Z = [6, 7]
        nb = len(kblocks)
        sb_ = {}

        def issue_qk(j):
            kb = kblocks[j]
            nk, c0 = kb["nk"], kb["c0"]
            sp = srotA[0] % 2
            srotA[0] += 1
            bs = [2 * sp, 2 * sp + 1]
            pebias = (kb["kind"] == "diag" and nk == 128)
            if kb["kind"] == "grp":
                for gi, mem in enumerate(kb["members"]):
                    for c in range(2):
                        P.op("pe", "matmul", reads=["qT_t"] + mem["rk"], writes=[pk(bs[c])],
                             out=banks[bs[c]][0:128, gi * nq:(gi + 1) * nq], lhsT=mem["kT"][64 * c:64 * c + 64, 0:128],
                             rhs=qT_t[64 * c:64 * c + 64, h, 0:nq], start=True, stop=True)
                sb_[j] = sp
                return
            for c in range(2):
                P.op("pe", "matmul", reads=["qT_t"] + kb["rk"], writes=[pk(bs[c])],
                     out=banks[bs[c]][0:nk, c0:nq], lhsT=kb["kT"][64 * c:64 * c + 64, 0:nk],
                     rhs=qT_t[64 * c:64 * c + 64, h, c0:nq], start=True, stop=(not pebias))
            if pebias:
                for c in range(2):
                    P.op("pe", "matmul", reads=["ident", "Ttab"], writes=[pk(bs[c])],
                         out=banks[bs[c]][0:nk, c0:c0 + 128], lhsT=ident[0:nk, 0:nk], rhs=Ttab[0:nk, h, 0:128],
                         start=False, stop=True)
            sb_[j] = sp

        issue_qk(0)
        if nb > 1:
            issue_qk(1)
        for j in range(nb):
            kb = kblocks[j]
            nk, c0 = kb["nk"], kb["c0"]
            sp = sb_.pop(j)
            bs = [2 * sp, 2 * sp + 1]
            pi = pctr[0] % 3
            pctr[0] += 1
            pt = ptile2[pi]
            bcol = btab[0:nk, kb["bcol"]:kb["bcol"] + 1]
            if kb["kind"] == "grp":
                ng = len(kb["members"])
                sin = psall[0:128, sp * 1024:(sp + 1) * 1024].rearrange("p (c n) -> p c n", c=2)[:, :, 0:ng * nq]
                pout = pt[0:128, :].rearrange("p (c n) -> p c n", c=2)[:, :, 0:ng * nq]
                P.op("act", "activation", reads=[pk(bs[0]), pk(bs[1]), "btab"], writes=["pt%d" % pi],
                     out=pout, in_=sin, func=AF.Exp, bias=bcol, scale=1.0)

                def pv_fn(kb=kb, ng=ng, pt=pt, pi=pi, j=j):
                    for gi, mem in enumerate(kb["members"]):
                        first = (j == 0 and gi == 0)
                        last = (j == nb - 1 and gi == ng - 1)
                        for c in range(2):
                            prhs = pt[0:128, c * 512 + gi * nq:c * 512 + (gi + 1) * nq]
                            P.op("pe", "matmul", reads=["pt%d" % pi] + mem["rv"], writes=[pk(O[c])],
                                 out=banks[O[c]][:, 0:nq], lhsT=mem["v"], rhs=prhs, start=first, stop=last)
                            P.op("pe", "matmul", reads=["pt%d" % pi, "onesw"], writes=[pk(Z[c])],
                                 out=banks[Z[c]][:, 0:nq], lhsT=mem["zl"], rhs=prhs, start=first, stop=last)
            elif kb["kind"] == "diag" and nk < 128:
                for c in range(2):
                    dg = dgb[0]
                    P.op("dve", "tensor_tensor", reads=[pk(bs[c]), "Ttab"], writes=["dg0"],
                         out=dg[0:nk, c0:nq], in0=banks[bs[c]][0:nk, c0:nq], in1=Ttab[0:nk, h, 0:nq - c0],
                         op=ALU.add)
                    P.op("act", "activation", reads=["dg0", "btab"], writes=["pt%d" % pi],
                         out=pt[0:nk, c * 512 + c0:c * 512 + nq], in_=dg[0:nk, c0:nq], func=AF.Exp, bias=bcol,
                         scale=1.0)
            else:
                sin = psall[0:nk, sp * 1024:(sp + 1) * 1024].rearrange("p (c n) -> p c n", c=2)[:, :, c0:nq]
                pout = pt[0:nk, :].rearrange("p (c n) -> p c n", c=2)[:, :, c0:nq]
                P.op("act", "activation", reads=[pk(bs[0]), pk(bs[1]), "btab"], writes=["pt%d" % pi],
                     out=pout, in_=sin, func=AF.Exp, bias=bcol, scale=1.0)
            if kb["kind"] != "grp":
                def pv_fn(kb=kb, nk=nk, c0=c0, pt=pt, pi=pi, j=j):
                    for c in range(2):
                        prhs = pt[0:nk, c * 512 + c0:c * 512 + nq]
                        P.op("pe", "matmul", reads=["pt%d" % pi] + kb["rv"], writes=[pk(O[c])],
                             out=banks[O[c]][:, c0:nq], lhsT=kb["v"], rhs=prhs, start=(j == 0), stop=(j == nb - 1))
                    for c in range(2):
                        prhs = pt[0:nk, c * 512 + c0:c * 512 + nq]
                        P.op("pe", "matmul", reads=["pt%d" % pi, "ones1"], writes=[pk(Z[c])],
                             out=banks[Z[c]][:, c0:nq], lhsT=ones1[0:nk, :], rhs=prhs, start=(j == 0),
                             stop=(j == nb - 1))
            if pending_fin is not None and j == min(9, nb - 1):
                if pending_fin[0] is not None:
                    pending_fin[0]()
                    pending_fin[0] = None
                pending_fin[1](bs[0])
                pending_fin = None
            if j + 2 < nb:
                issue_qk(j + 2)
            pv_fn()
            if pending_fin is not None and j == (0 if nb <= 4 else 1) and pending_fin[0] is not None:
                pending_fin[0]()
                pending_fin[0] = None
        fo_, fosq_, fms_, frs_ = fo, fosq, fz0, ft0
        fok = "fo"
        zin = psall[:, 6 * 512:8 * 512].rearrange("p (c n) -> p c n", c=2)[:, :, 0:nq]
        oin = psall[:, 4 * 512:6 * 512].rearrange("p (c n) -> p c n", c=2)[:, :, 0:nq]
        fz3 = fzz.rearrange("p (c n) -> p c n", c=2)[:, :, 0:nq]
        fo3 = fto.rearrange("p (c n) -> p c n", c=2)[:, :, 0:nq]
        P.op("dve", "tensor_copy", reads=[pk(4), pk(5)], writes=["ft0", fok], out=fo3, in_=oin)
        P.op("dve", "tensor_copy", reads=[pk(6), pk(7)], writes=["fz0", "fz1"], out=fz3, in_=zin)
        tc0 = tile_info(t)[0]

        def fin2a():
            if h == 0:
                P.op("dve", "reciprocal", reads=["fz0", "fz1"], writes=["fz0", "fz1"], out=fz3, in_=fz3)
            elif nb >= 8:
                P.op("dve", "reciprocal", reads=["fz0"], writes=["fz0"], out=fz0[:, 0:nq], in_=fz0[:, 0:nq])
                P.op("act", "activation", reads=["fz1"], writes=["fz1"], out=fz1[:, 0:nq], in_=fz1[:, 0:nq],
                     func=AF.Ln)
                P.op("act", "activation", reads=["fz1"], writes=["fz1"], out=fz1[:, 0:nq], in_=fz1[:, 0:nq],
                     func=AF.Exp, scale=-1.0)
            else:
                P.op("act", "activation", reads=["fz0", "fz1"], writes=["fz0", "fz1"], out=fz3, in_=fz3, func=AF.Ln)
                P.op("act", "activation", reads=["fz0", "fz1"], writes=["fz0", "fz1"], out=fz3, in_=fz3,
                     func=AF.Exp, scale=-1.0)
            P.op("dve", "tensor_tensor", reads=["ft0", "fz0"], writes=["ft0"], out=ft0[:, 0:nq], in0=ft0[:, 0:nq],
                 in1=fz0[:, 0:nq], op=ALU.mult)
            P.op("dve", "tensor_tensor", reads=[fok, "fz1"], writes=["fz1"], out=fz1[:, 0:nq], in0=fo_[:, 0:nq],
                 in1=fz1[:, 0:nq], op=ALU.mult)
            P.op("dve", "scalar_tensor_tensor", reads=["fz1", "ft0", "neglam"], writes=[fok], out=fo_[:, 0:nq],
                 in0=fz1[:, 0:nq], scalar=neglam[:, 0:1], in1=ft0[:, 0:nq], op0=ALU.mult, op1=ALU.add)
            P.op("pool", "tensor_tensor", reads=[fok], writes=["fosq"], out=fosq_[:, 0:nq], in0=fo_[:, 0:nq],
                 in1=fo_[:, 0:nq], op=ALU.mult)

        def fin2b(mb=7):
            P.op("pe", "matmul", reads=["fosq", "ones128"], writes=[pk(mb)], out=banks[mb][:, 0:nq],
                 lhsT=ones128[:], rhs=fosq_[:, 0:nq], start=True, stop=True)
            P.op("act", "activation", reads=[pk(mb), "epscol"], writes=["fz0"], out=fms_[:, 0:nq],
                 in_=banks[mb][:, 0:nq], func=AF.Ln, bias=epscol[:, 0:1], scale=1.0)
            P.op("act", "activation", reads=["fz0"], writes=["ft0"], out=frs_[:, 0:nq], in_=fms_[:, 0:nq],
                 func=AF.Exp, scale=-0.5)
            P.op("dve", "tensor_tensor", reads=[fok, "ft0"], writes=[fok], out=fo_[:, 0:nq], in0=fo_[:, 0:nq],
                 in1=frs_[:, 0:nq], op=ALU.mult)
            P.op("dve", "scalar_tensor_tensor", reads=[fok, "gsub", "szaT"], writes=["oaT%d_%d" % (t, h)],
                 out=oaT[:, h, tc0:tc0 + nq], in0=fo_[:, 0:nq], scalar=gsub[:, 0:1], in1=szaT[:, h, 0:nq],
                 op0=ALU.mult, op1=ALU.mult)
        return [fin2a, fin2b]

    mrot = [0]
    trot = [0]

    def a_stage1(b):
        c0, nt = blk_info(b)
        base = 3 * (mrot[0] % 2)
        mrot[0] += 1
        bq, bk_, bv = base, base + 1, base + 2
        mm_tm(bq, c0, nt, 0, "W0")
        mm_tm(bk_, c0, nt, 512, "W1")
        mm_tm(bv, c0, nt, 1024, "W2")
        return bq, bk_, bv

    blkctr = [0]

    def a_stage2(b, bi, bnk):
        c0, nt = blk_info(b)
        bq, bk_, bv = bnk
        par = blkctr[0] % 2
        blkctr[0] += 1
        iq, ik = 2 * par, 2 * par + 1
        vf = vfb[b % 2]
        ko = kob[b % 2]
        kf = knf[0]
        for bank, i in ((bq, iq), (bk_, ik)):
            P.op("act", "activation", reads=[pk(bank)], writes=["sq%d" % i], out=sq[i][0:nt, :],
                 in_=banks[bank][0:nt, :], func=AF.Square)
        P.op("act", "activation", reads=[pk(bv)], writes=["vfb%d" % (b % 2)], out=vf[0:nt, :],
             in_=banks[bv][0:nt, :], func=AF.Copy)
        dma("sp", v_s if b == 0 else v_p[(b - 1) * 128:b * 128, :], vf[0:nt, :], reads=["vfb%d" % (b % 2)])
        for i in (iq, ik):
            P.op("dve", "tensor_reduce", reads=["sq%d" % i], writes=["ss8%d" % i], out=ss8[i][0:nt, :],
                 in_=sq[i][0:nt, :].rearrange("p (g d) -> p g d", g=8), axis=AX.X, op=ALU.add)
        vdst = v_sbf[0:nt, :] if b == 0 else v_bf[:, b - 1, :]
        vkey = "v_sbf" if b == 0 else "vbf%d" % b
        P.op("dve", "tensor_copy", reads=[pk(bv)], writes=[vkey], out=vdst, in_=banks[bv][0:nt, :])
        for i in (iq, ik):
            P.op("act", "activation", reads=["ss8%d" % i, "epscol"], writes=["rs8a%d" % i], out=rs8[i][0:nt, :],
                 in_=ss8[i][0:nt, :], func=AF.Ln, bias=epscol[0:nt, 1:2], scale=1.0)
            P.op("act", "activation", reads=["rs8a%d" % i], writes=["rs8%d" % i], out=rs8[i][0:nt, :],
                 in_=rs8[i][0:nt, :], func=AF.Exp, scale=-0.5)
        rbq = rs8[iq][0:nt, :].unsqueeze(2).to_broadcast([nt, 8, 64])
        rbk = rs8[ik][0:nt, :].unsqueeze(2).to_broadcast([nt, 8, 64])
        P.op("dve", "tensor_tensor", reads=[pk(bq), "rs8%d" % iq], writes=["qnb0"],
             out=qnb[0][0:nt, :].rearrange("p (g d) -> p g d", g=8),
             in0=banks[bq][0:nt, :].rearrange("p (g d) -> p g d", g=8), in1=rbq, op=ALU.mult)
        P.op("dve", "tensor_tensor", reads=[pk(bk_), "rs8%d" % ik], writes=["knf0"],
             out=kf[0:nt, :].rearrange("p (g d) -> p g d", g=8),
             in0=banks[bk_][0:nt, :].rearrange("p (g d) -> p g d", g=8), in1=rbk, op=ALU.mult)
        pq = banks[6][:, 0:256].bitcast(BF)
        for h in range(4):
            P.op("pe", "transpose", reads=["qnb0", "ident"], writes=[pk(6)], out=pq[:, h * 128:h * 128 + nt],
                 in_=qnb[0][0:nt, h * 128:(h + 1) * 128], identity=ident[0:nt, 0:nt])
        P.op("act", "activation", reads=["knf0"], writes=["qnb1"], out=qnb[1][0:nt, :], in_=kf[0:nt, :],
             func=AF.Copy)
        P.op("dve", "tensor_tensor", reads=["knf0", "gkbc8"], writes=["kob%d" % (b % 2)], out=ko[0:nt, :],
             in0=kf[0:nt, :], in1=gkbc8[0:nt].rearrange("p a b -> p (a b)"), op=ALU.mult)
        dma("sp", k_s if b == 0 else k_p[(b - 1) * 128:b * 128, :], ko[0:nt, :], reads=["kob%d" % (b % 2)])
        pk_ = banks[7][:, 0:256].bitcast(BF)
        for h in range(4):
            P.op("pe", "transpose", reads=["qnb1", "ident"], writes=[pk(7)], out=pk_[:, h * 128:h * 128 + nt],
                 in_=qnb[1][0:nt, h * 128:(h + 1) * 128], identity=ident[0:nt, 0:nt])
        lq = bi * 128
        P.op("dve", "tensor_scalar", reads=[pk(6), "qkg"], writes=["qT_t"], out=qT_t[:, :, lq:lq + nt],
             in0=pq.rearrange("p (h t) -> p h t", h=4)[:, :, 0:nt], scalar1=qkg[:, 0:1], scalar2=None, op0=ALU.mult)
        if b == 0:
            dstT, key = kT_s[:, :, 0:nt], "kT_s"
        else:
            dstT, key = kT[:, :, (b - 1) * 128:(b - 1) * 128 + nt], "kT%d" % b
        P.op("dve", "tensor_scalar", reads=[pk(7), "qkg"], writes=[key], out=dstT,
             in0=pk_.rearrange("p (h t) -> p h t", h=4)[:, :, 0:nt], scalar1=qkg[:, 1:2], scalar2=None, op0=ALU.mult)

    pfin = [None]
    for t in range(5):
        tc0, nq, blks = tile_info(t)
        pend = a_stage1(blks[0])
        if pfin[0] is not None:
            pfin[0][0]()
            pfin[0][1]()
            pfin[0] = None
        for bi, b in enumerate(blks):
            cur = pend
            if bi + 1 < len(blks):
                pend = a_stage1(blks[bi + 1])
            a_stage2(b, bi, cur)
        if t == 0:
            ckpt("A0n")
        for h in range(4):
            bz = gbank()
            mm_fm(bz, tc0, nq, 1536 + 128 * h, "W3")
            silu_fm(bz, nq, szaT[:, h, 0:nq], "szaT", h)
        if t == 0:
            ckpt("A0z")
        if t == 4:
            load_w(2048)
        for h in range(4):
            kbl = []
            if t == 0:
                if h < 3:
                    load_cache_head(h + 1)
                kc_cur = kc_tms[h % 2]
                cv_cur = cvhs[h % 2]
                cvkey = "cvh%d" % (h % 2)
                for g in range(8):
                    tb = gbank()
                    pst = banks[tb][:, 0:256].bitcast(BF)
                    for u in range(4):
                        P.op("pe", "transpose", reads=["kc_tm%d" % (h % 2), "ident"], writes=[pk(tb)],
                             out=pst[:, u * 128:(u + 1) * 128], in_=kc_cur[:, g * 4 + u, :], identity=ident[:])
                    eng = "dve"
                    if eng == "dve":
                        P.op("dve", lambda e, pst=pst, g=g: e.tensor_copy(out=kcT[:, g * 512:(g + 1) * 512], in_=pst),
                             reads=[pk(tb)], writes=["kcT%d" % g])
                    else:
                        P.op("act", lambda e, pst=pst, g=g: e.activation(out=kcT[:, g * 512:(g + 1) * 512], in_=pst,
                                                                         func=AF.Copy), reads=[pk(tb)],
                             writes=["kcT%d" % g])
                cv4 = cv_flat[h % 2].rearrange("p (g r c) -> p g r c", r=4, c=128)
                for d in (1, 2, 3):
                    P.op("dve", "tensor_scalar", reads=[cvkey], writes=[cvkey], out=cv4[:, :, 3 - d, :],
                         in0=cv4[:, :, 3 - d, :], scalar1=float(math.exp(-SLOPES[h] * 128.0 * d)), scalar2=None,
                         op0=ALU.mult)
                for g in range(8):
                    mems = []
                    for j in range(4 * g, 4 * g + 4):
                        d = 4 * g + 3 - j
                        mems.append(dict(kT=kcT[:, j * 128:(j + 1) * 128], v=cv_cur[:, j, :], zl=onesw[:, h, d, :],
                                         rk=["kcT%d" % g], rv=[cvkey]))
                    kbl.append(dict(kind="grp", members=mems, nk=128, c0=0, bcol=BI[("sp", h, 4 * g + 3)]))
                kbl.append(dict(kT=kT_s[:, h, :], v=v_sbf[0:NS, h * 128:(h + 1) * 128], nk=NS, kind="diag",
                                bcol=BI[("sd", h)], c0=0, rk=["kT_s"], rv=["v_sbf"]))
            else:
                for j in range(4 * t):
                    bglob = 1 + j
                    if j < 4 * (t - 1):
                        m = 4 * (t - 1) - j
                        kbl.append(dict(kT=kT[:, h, j * 128:(j + 1) * 128], v=v_bf[:, j, h * 128:(h + 1) * 128],
                                        nk=128, kind="past", bcol=BI[("pp", h, m)], c0=0, rk=["kT%d" % bglob],
                                        rv=["vbf%d" % bglob]))
                    else:
                        m = j - 4 * (t - 1)
                        kbl.append(dict(kT=kT[:, h, j * 128:(j + 1) * 128], v=v_bf[:, j, h * 128:(h + 1) * 128],
                                        nk=128, kind="diag", bcol=BI[("pd", h, m)], c0=128 * m,
                                        rk=["kT%d" % bglob], rv=["vbf%d" % bglob]))
            if t == 0 and h == 0:
                ckpt("A0c")
            pfin[0] = attention_head(t, h, nq, kbl, pfin[0])
            if t == 0 and h == 3:
                pfin[0][0]()
                pfin[0][1]()
                pfin[0] = None
            if t == 0 and h == 0:
                ckpt("A0h")
        if t == 0:
            ckpt("As")
            P.barrier()
        if t == 1:
            ckpt("A1")
    if pfin[0] is not None:
        pfin[0][0]()
        pfin[0][1]()
        pfin[0] = None
    OAT_ALL = ["oaT%d_%d" % (t, h) for t in range(5) for h in range(4)]

    ckpt("A")
    P.barrier()
    treset()
    dma("pool", Woa, w_oa.rearrange("(k p) n -> p k n", p=128), writes=["Woa"])
    dma("pool", Wob, w_ob.rearrange("(k p) n -> p k n", p=128), writes=["Wob"])
    dma("pool", Wout[:, 0:4, :], w_out.rearrange("(k p) n -> p k n", p=128)[:, 0:4, :], writes=["Wout0"])
    dma("pool", Wout[:, 4:8, :], w_out.rearrange("(k p) n -> p k n", p=128)[:, 4:8, :], writes=["Wout1"])
    qbT = talloc("qbT", 2048, BF).rearrange("p (h t) -> p h t", h=4)
    qbxT = talloc("qbxT", 2048, BF).rearrange("p (h t) -> p h t", h=4)
    kbT = talloc("kbT", 2048, BF).rearrange("p (h t) -> p h t", h=4)
    szbT = talloc("szbT", 2048, BF).rearrange("p (h t) -> p h t", h=4)
    kb_tm = talloc("kb_tm", 2048, BF).rearrange("p (b c) -> p b c", c=512)
    vb_tm = talloc("vb_tm", 2048, BF).rearrange("p (b c) -> p b c", c=512)
    zv_tm = talloc("zv_tm", 2048, BF).rearrange("p (b c) -> p b c", c=512)
    thb = [talloc("thbB0", 512, F32)] * 2
    rst = talloc("rst", 4 * 8 * 128, BF).rearrange("p (h c e) -> p h c e", h=4, c=8)
    decbc = talloc("decbc", 512, F32).rearrange("p (h e) -> p h e", h=4)
    sdall = [talloc("sdall%d" % i, 512, BF).rearrange("p (b c) -> p b c", c=128) for i in range(2)]
    gsq = talloc("gsq", 512, BF)
    gms = talloc("gms", 512, F32)
    gto = talloc("gto", 512, F32)
    KSC = 128.0 ** -0.5

    dma("sp", r32[:], st.rearrange("h d e -> d h e"), writes=["r32"])
    for h in range(4):
        P.op("dve", "memset", writes=["decbc"], ap=decbc[:, h, :], constant=DECAY[h])
    sdc = [0]

    for t in range(5):
        tc0, nq, blks = tile_info(t)
        nch = nq // 64
        if t == 1:
            dma("sp", r_s.rearrange("h d e -> d h e"), r32[:], reads=["r32"])
            P.op("dve", "memset", writes=["r32"], ap=r32[:].rearrange("p h e -> p (h e)"), constant=0.0)
        for bi, b in enumerate(blks):
            c0, nt = blk_info(b)
            bk_ = gbank()
            mm_tm(bk_, c0, nt, 512, "W1")
            P.op("act", lambda e, bk_=bk_, bi=bi, nt=nt: e.mul(out=kb_tm[0:nt, bi, :], in_=banks[bk_][0:nt, :], mul=KSC), reads=[pk(bk_)],
                 writes=["kb_tm%d" % bi])
            bv = gbank()
            mm_tm(bv, c0, nt, 1024, "W2")
            P.op("act", lambda e, bv=bv, bi=bi, nt=nt: e.activation(out=vb_tm[0:nt, bi, :], in_=banks[bv][0:nt, :],
                                                                    func=AF.Copy), reads=[pk(bv)],
                 writes=["vb_tm%d" % bi])
            P.op("dve", lambda e, bv=bv, bi=bi, nt=nt: e.tensor_tensor(
                out=zv_tm[0:nt, bi, :].rearrange("p (h c) -> p h c", h=4),
                in0=banks[bv][0:nt, :].rearrange("p (h c) -> p h c", h=4),
                in1=zeta[0:nt, :].unsqueeze(2).to_broadcast([nt, 4, 128]), op=ALU.mult),
                reads=[pk(bv), "zeta"], writes=["zv_tm%d" % bi])
        def fm_head(h):
            bq = gbank()
            mm_fm(bq, tc0, nq, 0 + 128 * h, "W0")
            P.op("act", lambda e, bq=bq, h=h: e.activation(out=qbT[:, h, 0:nq], in_=banks[bq][:, 0:nq], func=AF.Copy),
                 reads=[pk(bq)], writes=["qbT"])
            P.op("dve", lambda e, bq=bq, h=h: e.tensor_tensor(
                out=qbxT[:, h, 0:nq].rearrange("p (n c) -> p n c", c=64),
                in0=banks[bq][:, 0:nq].rearrange("p (n c) -> p n c", c=64),
                in1=xi[:, h, :].unsqueeze(1).to_broadcast([128, nch, 64]), op=ALU.mult),
                reads=[pk(bq), "xi"], writes=["qbxT"])
            bk_ = gbank()
            mm_fm(bk_, tc0, nq, 512 + 128 * h, "W1")
            P.op("act", lambda e, bk_=bk_, h=h: e.mul(out=kbT[:, h, 0:nq], in_=banks[bk_][:, 0:nq], mul=KSC), reads=[pk(bk_)],
                 writes=["kbT"])
            bz = gbank()
            mm_fm(bz, tc0, nq, 1536 + 128 * h, "W3")
            silu_fm(bz, nq, szbT[:, h, 0:nq], "szbT", h)
        srot = [0]

        def stepA_chunk(ci):
            bi, cc = ci // 2, ci % 2
            ub = 4 + ci % 4
            for h in range(4):
                P.op("pe", "matmul", reads=["kb_tm%d" % bi, "zv_tm%d" % bi], writes=[pk(ub)],
                     out=banks[ub][:, h * 128:(h + 1) * 128],
                     lhsT=kb_tm[64 * cc:64 * cc + 64, bi, h * 128:(h + 1) * 128],
                     rhs=zv_tm[64 * cc:64 * cc + 64, bi, h * 128:(h + 1) * 128], start=True, stop=True)

        def stepB_chunk(ci):
            r32f = r32[:].rearrange("p h e -> p (h e)")
            ub = 4 + ci % 4
            P.op("dve", "tensor_tensor", reads=["r32", "decbc"], writes=["r32"], out=r32f, in0=r32f,
                 in1=decbc[:].rearrange("p h e -> p (h e)"), op=ALU.mult)
            P.op("dve", "tensor_tensor", reads=["r32", pk(ub)], writes=["r32"], out=r32f, in0=r32f,
                 in1=banks[ub][:, 0:512], op=ALU.add)
            if ci + 1 < nch:
                P.op("dve", "tensor_copy", reads=["r32"], writes=["rst_%d" % (ci + 1)], out=rst[:, :, ci + 1, :],
                     in_=r32[:])

        def stepC1(h):
            ob = 2 + (h % 2)
            sbk = h % 2
            nb_ = len(blks)
            nt = blk_info(blks[0])[1]
            for bi in range(nb_):
                lc = bi * 128
                P.op("pe", "matmul", reads=["kbT", "qbT"], writes=[pk(sbk)],
                     out=banks[sbk][0:nt, lc:lc + nt], lhsT=kbT[:, h, lc:lc + nt], rhs=qbT[:, h, lc:lc + nt],
                     start=True, stop=True)
            sd = sdall[h % 2]
            P.op("dve", "tensor_tensor", reads=[pk(sbk), "Dtab"], writes=["sdall%d" % (h % 2)],
                 out=sd[0:nt, 0:nb_, 0:nt],
                 in0=banks[sbk][0:nt, :].rearrange("p (b c) -> p b c", c=128)[:, 0:nb_, 0:nt],
                 in1=Dtab[0:nt, h, 0:nt].unsqueeze(1).to_broadcast([nt, nb_, nt]), op=ALU.mult)

        def stepC1b(h):
            ob = 2 + (h % 2)
            nb_ = len(blks)
            nt = blk_info(blks[0])[1]
            sd = sdall[h % 2]
            for bi in range(nb_):
                lc = bi * 128
                P.op("pe", "matmul", reads=["sdall%d" % (h % 2), "vb_tm%d" % bi], writes=[pk(ob)],
                     out=banks[ob][:, lc:lc + nt], lhsT=vb_tm[0:nt, bi, h * 128:(h + 1) * 128], rhs=sd[0:nt, bi, 0:nt],
                     start=True, stop=False)
                for cc in range(nt // 64):
                    ci = 2 * bi + cc
                    P.op("pe", "matmul", reads=["rst_%d" % ci, "qbxT"], writes=[pk(ob)],
                         out=banks[ob][:, lc + 64 * cc:lc + 64 * cc + 64], lhsT=rst[:, h, ci, :],
                         rhs=qbxT[:, h, lc + 64 * cc:lc + 64 * cc + 64], start=False, stop=(cc == nt // 64 - 1))

        def stepC2(h):
            ob = 2 + (h % 2)
            mb = 4 + h
            P.op("act", "activation", reads=[pk(ob)], writes=["gsq"], out=gsq[:, 0:nq], in_=banks[ob][:, 0:nq],
                 func=AF.Square)
            P.op("pe", "matmul", reads=["gsq", "ones128"], writes=[pk(mb)], out=banks[mb][:, 0:nq], lhsT=ones128[:],
                 rhs=gsq[:, 0:nq], start=True, stop=True)
            P.op("act", "activation", reads=[pk(mb), "epscol"], writes=["gms"], out=gms[:, 0:nq],
                 in_=banks[mb][:, 0:nq], func=AF.Ln, bias=epscol[:, 0:1], scale=1.0)
            P.op("act", "activation", reads=["gms"], writes=["gms"], out=gms[:, 0:nq], in_=gms[:, 0:nq], func=AF.Exp,
                 scale=-0.5)
            P.op("dve", "tensor_tensor", reads=[pk(ob), "gms"], writes=["gto"], out=gto[:, 0:nq],
                 in0=banks[ob][:, 0:nq], in1=gms[:, 0:nq], op=ALU.mult)
            P.op("dve", "scalar_tensor_tensor", reads=["gto", "szbT"], writes=["obT%d_%d" % (t, h)],
                 out=obT[:, h, tc0:tc0 + nq], in0=gto[:, 0:nq], scalar=0.5, in1=szbT[:, h, 0:nq], op0=ALU.mult,
                 op1=ALU.mult)

        for ci in range(min(4, nch)):
            stepA_chunk(ci)
        P.op("dve", "tensor_copy", reads=["r32"], writes=["rst_0"], out=rst[:, :, 0, :], in_=r32[:])
        for h in range(4):
            fm_head(h)
            if t == 4 and h == 3:
                load_w(4096)
            if h >= 1:
                for ci in (2 * (h - 1) + 4, 2 * (h - 1) + 5):
                    if ci < nch:
                        stepA_chunk(ci)
            for ci in (2 * h, 2 * h + 1):
                if ci < nch:
                    stepB_chunk(ci)
        stepC1(0)
        stepC1(1)
        stepC1b(0)
        stepC1(2)
        stepC1b(1)
        stepC2(0)
        stepC1(3)
        stepC1b(2)
        stepC2(1)
        stepC1b(3)
        stepC2(2)
        stepC2(3)
    dma("sp", r_p.rearrange("h d e -> d h e"), r32[:], reads=["r32"])
    OBT_ALL = ["obT%d_%d" % (t, h) for t in range(5) for h in range(4)]

    ckpt("B")
    P.barrier()
    treset()
    mT = talloc("mT", 4096, BF).rearrange("p (k t) -> p k t", k=8)
    tga = [talloc("tga%d" % i, 512, F32) for i in range(2)]
    tgb = [talloc("tgb%d" % i, 512, F32) for i in range(2)]
    u1 = [talloc("u1%d" % i, 512, F32) for i in range(2)]
    u2 = [talloc("u2%d" % i, 512, F32) for i in range(2)]
    xres = [talloc("xres%d" % i, 1024, F32) for i in range(2)]
    yb = [talloc("yb%d" % i, 1024, F32) for i in range(2)]
    allb = [0]

    def abank():
        b = allb[0] % 8
        allb[0] += 1
        return b

    for t in range(5):
        tc0, nq, blks = tile_info(t)
        for fb in range(8):
            i = fb % 2
            b1 = abank()
            mm_fm(b1, tc0, nq, 128 * fb, "W%d" % (fb // 4))
            P.op("act", lambda e, b1=b1, fb=fb, i=i: e.activation(out=tga[i][:, 0:nq], in_=banks[b1][:, 0:nq],
                                                                  func=AF.Tanh, bias=bgh[:, fb:fb + 1], scale=0.5),
                 reads=[pk(b1), "bgh"], writes=["tga%d" % i])
            b2 = abank()
            mm_fm(b2, tc0, nq, 1024 + 128 * fb, "W%d" % (2 + fb // 4))
            P.op("act", lambda e, b2=b2, fb=fb, i=i: e.activation(out=tgb[i][:, 0:nq], in_=banks[b2][:, 0:nq],
                                                                  func=AF.Tanh, bias=bgh[:, 8 + fb:9 + fb], scale=0.5),
                 reads=[pk(b2), "bgh"], writes=["tgb%d" % i])
            b3 = abank()
            for kc in range(4):
                P.op("pe", lambda e, b3=b3, kc=kc, fb=fb: e.matmul(
                    banks[b3][:, 0:nq], lhsT=Woa[:, kc, fb * 128:(fb + 1) * 128], rhs=oaT[:, kc, tc0:tc0 + nq],
                    start=(kc == 0), stop=(kc == 3)), reads=["Woa"] + OAT_ALL, writes=[pk(b3)])
            b4 = abank()
            for kc in range(4):
                P.op("pe", lambda e, b4=b4, kc=kc, fb=fb: e.matmul(
                    banks[b4][:, 0:nq], lhsT=Wob[:, kc, fb * 128:(fb + 1) * 128], rhs=obT[:, kc, tc0:tc0 + nq],
                    start=(kc == 0), stop=(kc == 3)), reads=["Wob"] + OBT_ALL, writes=[pk(b4)])
            P.op("dve", lambda e, b3=b3, i=i: e.scalar_tensor_tensor(
                out=u1[i][:, 0:nq], in0=tga[i][:, 0:nq], scalar=1.0, in1=banks[b3][:, 0:nq], op0=ALU.add,
                op1=ALU.mult), reads=["tga%d" % i, pk(b3)], writes=["u1%d" % i])
            P.op("dve", lambda e, b4=b4, i=i: e.scalar_tensor_tensor(
                out=u2[i][:, 0:nq], in0=tgb[i][:, 0:nq], scalar=1.0, in1=banks[b4][:, 0:nq], op0=ALU.add,
                op1=ALU.mult), reads=["tgb%d" % i, pk(b4)], writes=["u2%d" % i])
            P.op("pool", lambda e, i=i, fb=fb: e.tensor_tensor(out=mT[:, fb, 0:nq], in0=u1[i][:, 0:nq],
                                                               in1=u2[i][:, 0:nq], op=ALU.add),
                 reads=["u1%d" % i, "u2%d" % i], writes=["mT%d" % fb])
        for bi, b in enumerate(blks):
            c0, nt = blk_info(b)
            i = b % 2
            src = x_s if b == 0 else x_p[(b - 1) * 128:b * 128, :]
            dma("sp", xres[i][0:nt, :], src, writes=["xres%d" % i])
            for half in range(2):
                by = abank()
                for kc in range(8):
                    P.op("pe", lambda e, by=by, kc=kc, bi=bi, nt=nt, half=half: e.matmul(
                        banks[by][0:nt, :], lhsT=mT[:, kc, bi * 128:bi * 128 + nt],
                        rhs=Wout[:, kc, half * 512:(half + 1) * 512], start=(kc == 0), stop=(kc == 7)),
                        reads=["mT%d" % kc, "Wout%d" % (kc // 4)], writes=[pk(by)])
                P.op("dve", lambda e, by=by, i=i, nt=nt, half=half: e.scalar_tensor_tensor(
                    out=yb[i][0:nt, half * 512:(half + 1) * 512], in0=banks[by][0:nt, :], scalar=0.5,
                    in1=xres[i][0:nt, half * 512:(half + 1) * 512], op0=ALU.mult, op1=ALU.add),
                    reads=[pk(by), "xres%d" % i], writes=["yb%d_%d" % (i, half)])
            dst = y_s if b == 0 else y_p[(b - 1) * 128:b * 128, :]
            dma("sp", dst, yb[i][0:nt, :], reads=["yb%d_0" % i, "yb%d_1" % i])
    P.emit()
    return nc, P, bias_tab_np


_CACHE = {}


def _get_program():
    if "p" not in _CACHE:
        _CACHE["p"] = build_program()
    return _CACHE["p"]


def kernel(x_prompt, x_sample, cache_k_diff, cache_v_diff, state_ret, norm_g, w_in, b_gate, qn_g, kn_g,
           lam_q1, lam_k1, lam_q2, lam_k2, subln_g, w_o_diff, w_o_ret, w_out):
    f = lambda a: np.ascontiguousarray(np.asarray(a, dtype=np.float32))
    nc, P, bias_tab = _get_program()
    T, Dm, zeta, xi = _const_tables()
    qn = f(qn_g).reshape(64)
    kn = f(kn_g).reshape(64)
    shared = {
        "w_in": f(w_in)[0], "w_oa": f(w_o_diff)[0], "w_ob": f(w_o_ret)[0], "w_out": f(w_out)[0],
        "norm_g": f(norm_g).reshape(D),
        "bgT": np.ascontiguousarray(f(b_gate).reshape(16, 128).T),
        "qk_g": np.ascontiguousarray(np.stack([np.tile(qn, 2), np.tile(kn, 2)], axis=1)),
        "kn_g": kn,
        "lamv": np.ascontiguousarray(np.stack([f(lam_q1).reshape(64), f(lam_k1).reshape(64),
                                               f(lam_q2).reshape(64), f(lam_k2).reshape(64)])),
        "subg": f(subln_g).reshape(128, 1),
        "c_ident": np.eye(128, dtype=np.float32),
        "c_T": T.reshape(128, -1), "c_D": Dm.reshape(128, -1), "c_zeta": zeta, "c_xi": xi.reshape(128, -1),
        "c_bias": bias_tab,
    }
    xp, xs = f(x_prompt), f(x_sample)
    ckd, cvd, srt = f(cache_k_diff), f(cache_v_diff), f(state_ret)
    in_maps = []
    for b in range(8):
        m = dict(shared)
        m["x_p"] = xp[b]
        m["x_s"] = xs[b]
        m["ck"] = ckd[0, b].reshape(PAST, 512)
        m["cv"] = cvd[0, b].reshape(PAST, 512)
        m["st"] = srt[0, b]
        in_maps.append(m)
    res = run_bass_kernel_spmd(nc, in_maps, core_ids=list(range(8)))
    R = res.results
    y_prompt = np.stack([R[b]["y_p"] for b in range(8)])
    y_sample = np.stack([R[b]["y_s"] for b in range(8)])
    k_prompt = np.stack([R[b]["k_p"] for b in range(8)]).reshape(1, 8, SEQ, 4, 2, 64)
    v_prompt = np.stack([R[b]["v_p"] for b in range(8)]).reshape(1, 8, SEQ, 4, 128)
    ret_prompt = np.stack([R[b]["r_p"] for b in range(8)]).reshape(1, 8, 4, 128, 128)
    k_sample = np.stack([R[b]["k_s"] for b in range(8)]).reshape(1, 8, NS, 4, 2, 64)
    v_sample = np.stack([R[b]["v_s"] for b in range(8)]).reshape(1, 8, NS, 4, 128)
    ret_sample = np.stack([R[b]["r_s"] for b in range(8)]).reshape(1, 8, 4, 128, 128)
    return (y_prompt.astype(np.float32), y_sample.astype(np.float32), k_prompt.astype(np.float32),
            v_prompt.astype(np.float32), ret_prompt.astype(np.float32), k_sample.astype(np.float32),
            v_sample.astype(np.float32), ret_sample.astype(np.float32))
```

```python
import math
import numpy as np
import concourse.bass as bass
import concourse.mybir as mybir
from concourse.bass_utils import run_bass_kernel_spmd

F32 = mybir.dt.float32
BF = mybir.dt.bfloat16
AF = mybir.ActivationFunctionType
ALU = mybir.AluOpType
AX = mybir.AxisListType

D = 1024
SEQ = 2048
NS = 64
PAST = 4096
NTOK = NS + SEQ
EPS = 1e-6
LAM_INIT = 0.8 - 0.6 * math.exp(0.0)
SLOPES = [2.0 ** (-8.0 * h / 4) for h in range(1, 5)]
LOGG = [math.log(1.0 - 2.0 ** (-5.0 - h)) for h in range(4)]
DECAY = [math.exp(64 * g) for g in LOGG]
NEG = -30000.0

ENGS = ("pe", "act", "dve", "pool", "sp")


class _Rec:
    def __init__(self):
        self.call = None

    def __getattr__(self, name):
        def f(*a, **k):
            self.call = (name, a, k)
            return None
        return f


class Prog:
    def __init__(self, nc, n_dma_sems=24):
        self.nc = nc
        self.ops = []
        self.last_writer = {}
        self.readers = {}
        self.pending_bar = {}
        self.n_dma_sems = n_dma_sems
        self.last_on_eng = {}
        self.dma_since_bar = []

    def op(self, eng, fn, reads=(), writes=(), dma=False, **kw):
        if isinstance(fn, str):
            name = fn
            fn = (lambda e, name=name, kw=kw: getattr(e, name)(**kw))
        else:
            rec = _Rec()
            fn(rec)
            rname, rargs, rkw = rec.call
            fn = (lambda e, rname=rname, rargs=rargs, rkw=rkw: getattr(e, rname)(*rargs, **rkw))
        idx = len(self.ops)
        deps = set()
        for k in reads:
            w = self.last_writer.get(k)
            if w is not None:
                deps.add((w, "raw"))
            if k.startswith("ps"):
                for e2, r in self.readers.get(k, {}).items():
                    if e2 != eng and not isinstance(r, list):
                        deps.add((r, "raw"))
        for k in writes:
            w = self.last_writer.get(k)
            if w is not None:
                deps.add((w, "waw"))
            for r in self.readers.get(k, {}).values():
                if isinstance(r, list):
                    for rr in r:
                        deps.add((rr, "war"))
                else:
                    deps.add((r, "war"))
        for k in writes:
            self.last_writer[k] = idx
            self.readers[k] = {}
        for k in reads:
            rd = self.readers.setdefault(k, {})
            if dma:
                rd.setdefault("dma", []).append(idx)
            else:
                rd[eng] = idx
        if eng in self.pending_bar:
            for d in self.pending_bar.pop(eng):
                deps.add((d, "raw"))
        self.ops.append(dict(eng=eng, fn=fn, deps=deps, dma=dma, idx=idx))
        if dma:
            self.dma_since_bar.append(idx)
        else:
            self.last_on_eng[eng] = idx
        return idx

    def barrier(self):
        frontier = set(self.last_on_eng.values()) | set(self.dma_since_bar)
        self.dma_since_bar = []
        self.last_writer = {}
        self.readers = {}
        for e in ENGS:
            cur = self.pending_bar.get(e, set())
            self.pending_bar[e] = cur | frontier

    def emit(self):
        nc = self.nc
        ops = self.ops
        pos = {}
        cnt = {}
        for o in ops:
            if not o["dma"]:
                c = cnt.get(o["eng"], 0)
                pos[o["idx"]] = c
                cnt[o["eng"]] = c + 1
        for o in ops:
            real = set()
            for (d, kind) in o["deps"]:
                p = ops[d]
                if d == o["idx"]:
                    continue
                if not o["dma"] and not p["dma"] and p["eng"] == o["eng"]:
                    if o["eng"] == "pe":
                        continue
                    if pos[o["idx"]] - pos[d] > 6:
                        continue
                real.add(d)
            o["rdeps"] = real
        needs = [False] * len(ops)
        for o in ops:
            for d in o["rdeps"]:
                needs[d] = True
        sems = {}
        nd = self.n_dma_sems // 2
        names = list(ENGS) + ["dmaS%d" % i for i in range(nd)] + ["dmaP%d" % i for i in range(nd)]
        cms = []
        for n in names:
            cm = nc.semaphore("s_" + n)
            sems[n] = cm.__enter__()
            cms.append(cm)
        count = {n: 0 for n in names}
        dma_i = {"sp": 0, "pool": 0}
        for o in ops:
            if o["dma"]:
                q = o["eng"]
                n = ("dmaS%d" if q == "sp" else "dmaP%d") % (dma_i[q] % nd)
                dma_i[q] += 1
                o["prev_same_sem"] = (n, count[n])
                count[n] += 16
                o["sig"] = (n, count[n])
            elif needs[o["idx"]]:
                n = o["eng"]
                count[n] += 1
                o["sig"] = (n, count[n])
            else:
                o["sig"] = None
        final_counts = dict(count)
        by_eng = {e: [o for o in ops if o["eng"] == e] for e in ENGS}

        def run_engine(ename, eng, final=False):
            waited = {}

            def wait(n, v):
                if v <= 0 or waited.get(n, 0) >= v:
                    return
                eng.wait_ge(sems[n], v)
                waited[n] = v

            for o in by_eng[ename]:
                need = {}
                for d in o["rdeps"]:
                    n, v = ops[d]["sig"]
                    if need.get(n, 0) < v:
                        need[n] = v
                if o["dma"]:
                    n, v = o["prev_same_sem"]
                    if need.get(n, 0) < v:
                        need[n] = v
                for n, v in need.items():
                    wait(n, v)
                ins = o["fn"](eng)
                if o["sig"] is not None:
                    n, v = o["sig"]
                    ins.then_inc(sems[n], 16 if o["dma"] else 1)
            if final:
                for n in names:
                    if n.startswith("dma"):
                        wait(n, final_counts[n])

        with nc.Block() as block:
            @block.tensor
            def _(e):
                run_engine("pe", e)

            @block.scalar
            def _(e):
                run_engine("act", e)

            @block.vector
            def _(e):
                run_engine("dve", e)

            @block.gpsimd
            def _(e):
                run_engine("pool", e)

            @block.sync
            def _(e):
                run_engine("sp", e, final=True)
        for cm in reversed(cms):
            cm.__exit__(None, None, None)
        self.stats = {e: len(by_eng[e]) for e in ENGS}
        self.stats["signals"] = {k: v for k, v in final_counts.items() if not k.startswith("dma")}


def _bias_cols():
    cols = []
    index = {}
    p = np.arange(128, dtype=np.float64)

    def add(key, vec):
        index[key] = len(cols)
        cols.append(np.asarray(vec, dtype=np.float64) * np.ones(128))

    for h in range(4):
        s = SLOPES[h]
        for m in range(1, 13):
            add(("pp", h, m), s * (p - 128.0 * m - 256.0))
        for m in range(4):
            add(("pd", h, m), s * (p + 128.0 * m - 256.0))
        for j in range(32):
            add(("sp", h, j), s * (128.0 * j + p - (PAST + 32.0)))
        add(("sd", h), s * (p - 32.0))
    return np.stack(cols, axis=1).astype(np.float32), index


def _const_tables():
    p = np.arange(128)[:, None]
    jj = np.arange(512)[None, :]
    T = np.zeros((128, 4, 512), np.float32)
    for h in range(4):
        s = SLOPES[h]
        val = s * (jj - np.abs(jj - p) - p)
        ok = (p // 64) <= (jj // 64)
        T[:, h, :] = np.where(ok, val, NEG)
    Dm = np.zeros((128, 4, 128), np.float32)
    t = np.arange(128)[None, :]
    sidx = np.arange(128)[:, None]
    for h in range(4):
        d = np.exp(np.abs(t - sidx) * LOGG[h])
        Dm[:, h, :] = np.where((t // 64) == (sidx // 64), d, 0.0)
    zeta = np.zeros((128, 4), np.float32)
    xi = np.zeros((128, 4, 64), np.float32)
    for h in range(4):
        zeta[:, h] = np.exp((63 - (np.arange(128) % 64)) * LOGG[h])
        xi[:, h, :] = np.exp((np.arange(64) + 1.0) * LOGG[h])[None, :]
    return T, Dm, zeta, xi


class _Stop(Exception):
    pass


def build_program(stop_at=None):
    import os
    stop_at = stop_at if stop_at is not None else os.environ.get("KSTOP")
    nc_holder = {}
    try:
        return _build_program(stop_at, nc_holder)
    except _Stop:
        nc, P, bt = nc_holder["v"]
        P.emit()
        return nc, P, bt


def _build_program(stop_at, nc_holder):
    nc = bass.Bass("TRN2", target_bir_lowering=False)
    P = Prog(nc)
    bias_tab_np, BI = _bias_cols()
    NBC = bias_tab_np.shape[1]
    nc_holder["v"] = (nc, P, bias_tab_np)

    def ckpt(name):
        if stop_at is not None and stop_at == name:
            raise _Stop()

    def din(name, shape):
        return nc.dram_tensor(name, list(shape), F32, kind="ExternalInput").ap()

    def dout(name, shape):
        return nc.dram_tensor(name, list(shape), F32, kind="ExternalOutput").ap()

    x_p = din("x_p", [SEQ, D])
    x_s = din("x_s", [NS, D])
    ck = din("ck", [PAST, 512])
    cv = din("cv", [PAST, 512])
    st = din("st", [4, 128, 128])
    w_in = din("w_in", [D, 6144])
    w_oa = din("w_oa", [512, D])
    w_ob = din("w_ob", [512, D])
    w_out = din("w_out", [D, D])
    norm_g = din("norm_g", [D])
    bgT_d = din("bgT", [128, 16])
    qk_g = din("qk_g", [128, 2])
    kn_g = din("kn_g", [64])
    lamv = din("lamv", [4, 64])
    subg = din("subg", [128, 1])
    c_ident = din("c_ident", [128, 128])
    c_T = din("c_T", [128, 4 * 512])
    c_D = din("c_D", [128, 4 * 128])
    c_zeta = din("c_zeta", [128, 4])
    c_xi = din("c_xi", [128, 4 * 64])
    c_bias = din("c_bias", [128, NBC])

    y_p = dout("y_p", [SEQ, D])
    y_s = dout("y_s", [NS, D])
    k_p = dout("k_p", [SEQ, 512])
    v_p = dout("v_p", [SEQ, 512])
    r_p = dout("r_p", [4, 128, 128])
    k_s = dout("k_s", [NS, 512])
    v_s = dout("v_s", [NS, 512])
    r_s = dout("r_s", [4, 128, 128])

    def sb(name, shape, dt):
        return nc.alloc_sbuf_tensor(name, list(shape), dt)

    XNT = sb("XNT", [128, 8, NTOK], BF)
    W = sb("W", [128, 8, 2048], BF)
    KV = sb("KV", [128, 16384], BF)
    oaT = sb("oaT", [128, 4, NTOK], BF)
    obT = sb("obT", [128, 4, NTOK], BF)
    ident = sb("ident", [128, 128], BF)
    ones1 = sb("ones1", [128, 128], BF)
    ones128 = sb("ones128", [128, 128], BF)
    onesw = sb("onesw", [128, 4, 4, 128], BF)
    Ttab = sb("Ttab", [128, 4, 512], BF)
    Dtab = sb("Dtab", [128, 4, 128], F32)
    zeta = sb("zeta", [128, 4], F32)
    xi = sb("xi", [128, 4, 64], F32)
    btab = sb("btab", [128, NBC], F32)
    bgh = sb("bgh", [128, 16], F32)
    qkg = sb("qkg", [128, 2], F32)
    gkbc8 = sb("gkbc8", [128, 8, 64], F32)
    gsub = sb("gsub", [128, 1], F32)
    neglam = sb("neglam", [128, 1], F32)
    nhalf = sb("nhalf", [128, 8], F32)
    epscol = sb("epscol", [128, 3], F32)
    lamt = sb("lamt", [128, 4, 64], F32)
    lams = sb("lams", [128, 4], F32)
    r32 = sb("r32", [128, 4, 128], F32)
    ssx = sb("ssx", [128, 17], F32)
    rsx = sb("rsx", [128, 17], F32)
    kT_s = sb("kT_s", [128, 4, NS], BF)
    v_sbf = sb("v_sbf", [128, 512], BF)
    TMP = sb("TMP", [128, 24576], BF)

    kT = KV[:, 0:8192].rearrange("p (h t) -> p h t", h=4)
    v_bf = KV[:, 8192:16384].rearrange("p (b c) -> p b c", c=512)
    kc_tm = KV[:, 0:4096].rearrange("p (b c) -> p b c", c=128)
    kcT = KV[:, 4096:8192]
    cvh = KV[:, 8192:12288].rearrange("p (b c) -> p b c", c=128)
    cvh2t = sb("cvh2", [128, 4096], BF)
    cv_flat = [KV[:, 8192:12288], cvh2t[:, :]]
    cvhs = [cv_flat[0].rearrange("p (b c) -> p b c", c=128), cv_flat[1].rearrange("p (b c) -> p b c", c=128)]
    kc_tms = [KV[:, 0:4096].rearrange("p (b c) -> p b c", c=128),
              KV[:, 12288:16384].rearrange("p (b c) -> p b c", c=128)]

    def load_cache_head(hh):
        dma("pool", kc_tms[hh % 2], ck.rearrange("(b p) c -> p b c", p=128)[:, :, hh * 128:(hh + 1) * 128],
            writes=["kc_tm%d" % (hh % 2)])
        dma("pool", cvhs[hh % 2], cv.rearrange("(b p) c -> p b c", p=128)[:, :, hh * 128:(hh + 1) * 128],
            writes=["cvh%d" % (hh % 2)])
    WO = KV
    Woa = KV[:, 0:4096].rearrange("p (k n) -> p k n", k=4)
    Wob = KV[:, 4096:8192].rearrange("p (k n) -> p k n", k=4)
    Wout = KV[:, 8192:16384].rearrange("p (k n) -> p k n", k=8)

    psall = nc.alloc_psum_tensor("psall", [128, 4096], F32)
    banks = [psall[:, i * 512:(i + 1) * 512] for i in range(8)]
    grot = [0]

    def gbank():
        b = grot[0] % 4
        grot[0] += 1
        return b

    hrot = [0]

    def hbank():
        b = 4 + hrot[0] % 4
        hrot[0] += 1
        return b

    def pk(b):
        return "ps%d" % b

    tmp_off = [0]

    def talloc(name, ncols, dt):
        n = ncols * (2 if dt == F32 else 1)
        o = tmp_off[0]
        assert o + n <= 24576, ("TMP overflow", name)
        tmp_off[0] = o + n
        ap = TMP[:, o:o + n]
        if dt == F32:
            ap = ap.bitcast(F32)
        return ap

    def treset():
        tmp_off[0] = 0

    def dma(eng, out, in_, reads=(), writes=()):
        P.op(eng, lambda e: e.dma_start(out=out, in_=in_), reads=reads, writes=writes, dma=True)

    dma("pool", ident[:], c_ident, writes=["ident"])
    dma("pool", Ttab[:].rearrange("p h c -> p (h c)"), c_T, writes=["Ttab"])
    dma("sp", Dtab[:].rearrange("p h c -> p (h c)"), c_D, writes=["Dtab"])
    dma("sp", zeta[:], c_zeta, writes=["zeta"])
    dma("sp", xi[:].rearrange("p h c -> p (h c)"), c_xi, writes=["xi"])
    dma("sp", btab[:], c_bias, writes=["btab"])
    dma("sp", bgh[:], bgT_d, writes=["bgh"])
    dma("sp", qkg[:], qk_g, writes=["qkg"])
    dma("sp", gsub[:], subg, writes=["gsub"])
    dma("sp", gkbc8[:, 0, :], kn_g.partition_broadcast(128), writes=["gkbc8"])
    for i in range(4):
        dma("sp", lamt[:, i, :], lamv[i].partition_broadcast(128), writes=["lamt"])
    P.op("dve", lambda e: e.memset(ones1[:], 1.0), writes=["ones1"])
    P.op("dve", lambda e: e.memset(ones128[:], 1.0 / 128), writes=["ones128"])
    for hh in range(4):
        for dd in range(4):
            P.op("dve", "memset", writes=["onesw"], ap=onesw[:, hh, dd, :],
                 constant=float(math.exp(-SLOPES[hh] * 128.0 * dd)))
    P.op("dve", lambda e: e.memset(nhalf[:], -0.5), writes=["nhalf"])
    P.op("dve", lambda e: e.memset(epscol[:, 0:1], EPS), writes=["epscol0"])
    P.op("dve", lambda e: e.memset(epscol[:, 1:2], 64 * EPS), writes=["epscol1"])
    P.op("dve", lambda e: e.memset(epscol[:, 2:3], 1024 * EPS), reads=["epscol0", "epscol1"], writes=["epscol"])
    P.op("dve", lambda e: e.memset(ssx[:], 0.0), writes=["ssx%d" % b for b in range(17)])
    P.op("dve", lambda e: e.tensor_scalar(out=bgh[:], in0=bgh[:], scalar1=0.5, scalar2=None, op0=ALU.mult),
         reads=["bgh"], writes=["bgh"])
    P.op("dve", lambda e: e.tensor_scalar(out=qkg[:, 1:2], in0=qkg[:, 1:2], scalar1=8.0, scalar2=None, op0=ALU.mult),
         reads=["qkg"], writes=["qkg"])
    P.op("dve", lambda e: e.tensor_scalar(out=gsub[:], in0=gsub[:], scalar1=(1.0 - LAM_INIT) * 0.5, scalar2=None,
                                          op0=ALU.mult), reads=["gsub"], writes=["gsub"])
    for j in range(1, 8):
        P.op("dve", lambda e, j=j: e.tensor_copy(out=gkbc8[:, j, :], in_=gkbc8[:, 0, :]), reads=["gkbc8"],
             writes=["gkbc8_%d" % j])
    P.op("dve", lambda e: e.tensor_scalar(out=gkbc8[:].rearrange("p a b -> p (a b)"),
                                          in0=gkbc8[:].rearrange("p a b -> p (a b)"), scalar1=8.0, scalar2=None,
                                          op0=ALU.mult), reads=["gkbc8"] + ["gkbc8_%d" % j for j in range(1, 8)],
         writes=["gkbc8"])
    P.op("dve", lambda e: e.tensor_tensor(out=lamt[:, 0, :], in0=lamt[:, 0, :], in1=lamt[:, 1, :], op=ALU.mult),
         reads=["lamt"], writes=["lamt"])
    P.op("dve", lambda e: e.tensor_tensor(out=lamt[:, 2, :], in0=lamt[:, 2, :], in1=lamt[:, 3, :], op=ALU.mult),
         reads=["lamt"], writes=["lamt"])
    P.op("dve", lambda e: e.tensor_reduce(out=lams[:, 0:1], in_=lamt[:, 0, :], axis=AX.X, op=ALU.add),
         reads=["lamt"], writes=["lams0"])
    P.op("dve", lambda e: e.tensor_reduce(out=lams[:, 1:2], in_=lamt[:, 2, :], axis=AX.X, op=ALU.add),
         reads=["lamt"], writes=["lams1"])
    P.op("act", lambda e: e.activation(out=lams[:, 2:4], in_=lams[:, 0:2], func=AF.Exp), reads=["lams0", "lams1"],
         writes=["lams2"])
    P.op("dve", lambda e: e.tensor_tensor(out=neglam[:], in0=lams[:, 3:4], in1=lams[:, 2:3], op=ALU.subtract),
         reads=["lams2"], writes=["neglam"])
    P.op("dve", lambda e: e.tensor_scalar(out=neglam[:], in0=neglam[:], scalar1=-LAM_INIT, scalar2=None, op0=ALU.add),
         reads=["neglam"], writes=["neglam"])

    def blk_info(b):
        if b == 0:
            return 0, NS
        return NS + (b - 1) * 128, 128

    def load_w(col0):
        for g in range(4):
            dma("pool", W[:, :, g * 512:(g + 1) * 512],
                w_in.rearrange("(kc p) n -> p kc n", p=128)[:, :, col0 + g * 512:col0 + (g + 1) * 512],
                writes=["W%d" % g])

    load_w(0)
    treset()
    gbc = talloc("gbc", 1024, F32)
    xin = [talloc("xin%d" % i, 1024, F32) for i in range(4)]
    junk = talloc("junk", 1024, BF)
    xnb = [talloc("xnb%d" % i, 1024, BF) for i in range(4)]
    dma("sp", gbc, norm_g.partition_broadcast(128), writes=["gbc"])
    P.op("dve", lambda e: e.tensor_scalar(out=gbc, in0=gbc, scalar1=32.0, scalar2=None, op0=ALU.mult),
         reads=["gbc"], writes=["gbc"])
    p0_pending = []
    for b in range(17):
        c0, nt = blk_info(b)
        xi_ = xin[b % 4]
        xb = xnb[b % 4]
        src = x_s if b == 0 else x_p[(b - 1) * 128:b * 128, :]
        dma("sp", xi_[0:nt, :], src, writes=["xin%d" % (b % 4)])
        P.op("act", lambda e, xi_=xi_, nt=nt, b=b: e.activation(out=junk[0:nt, :], in_=xi_[0:nt, :], func=AF.Square,
                                                               accum_out=ssx[0:nt, b:b + 1]),
             reads=["xin%d" % (b % 4)], writes=["junk", "ssx%d" % b])
        P.op("act", lambda e, nt=nt, b=b: e.activation(out=rsx[0:nt, b:b + 1], in_=ssx[0:nt, b:b + 1], func=AF.Ln,
                                                      bias=epscol[0:nt, 2:3], scale=1.0),
             reads=["ssx%d" % b, "epscol"], writes=["rsxa%d" % b])
        P.op("act", lambda e, nt=nt, b=b: e.activation(out=rsx[0:nt, b:b + 1], in_=rsx[0:nt, b:b + 1], func=AF.Exp,
                                                      scale=-0.5), reads=["rsxa%d" % b], writes=["rsx%d" % b])
        P.op("dve", lambda e, xi_=xi_, xb=xb, nt=nt, b=b: e.scalar_tensor_tensor(
            out=xb[0:nt, :], in0=xi_[0:nt, :], scalar=rsx[0:nt, b:b + 1], in1=gbc[0:nt, :], op0=ALU.mult,
            op1=ALU.mult), reads=["xin%d" % (b % 4), "rsx%d" % b, "gbc"], writes=["xnb%d" % (b % 4)])
        while p0_pending:
            P.op("dve", "tensor_copy", **p0_pending.pop(0))
        bk = gbank()
        pst = banks[bk][:, 0:512].bitcast(BF)
        for kc in range(8):
            P.op("pe", lambda e, pst=pst, xb=xb, nt=nt, kc=kc: e.transpose(
                out=pst[:, kc * 128:kc * 128 + nt], in_=xb[0:nt, kc * 128:(kc + 1) * 128], identity=ident[0:nt, 0:nt]),
                reads=["xnb%d" % (b % 4), "ident"], writes=[pk(bk)])
        p0_pending.append(dict(reads=[pk(bk)], writes=["XNT%d" % b], out=XNT[:, :, c0:c0 + nt],
                               in_=pst.rearrange("p (k t) -> p k t", k=8)[:, :, 0:nt]))
    while p0_pending:
        P.op("dve", "tensor_copy", **p0_pending.pop(0))
    XNT_ALL = ["XNT%d" % b for b in range(17)]
    ckpt("p0")

    def tile_info(t):
        if t == 0:
            return 0, NS, [0]
        return NS + (t - 1) * 512, 512, [1 + 4 * (t - 1) + i for i in range(4)]

    def mm_tm(bk, c0, nt, wc0, wkey):
        for kc in range(8):
            P.op("pe", lambda e, kc=kc: e.matmul(banks[bk][0:nt, :], lhsT=XNT[:, kc, c0:c0 + nt],
                                                  rhs=W[:, kc, wc0:wc0 + 512], start=(kc == 0), stop=(kc == 7)),
                 reads=XNT_ALL + [wkey], writes=[pk(bk)])

    def mm_fm(bk, c0, nq, wc0, wkey):
        for kc in range(8):
            P.op("pe", lambda e, kc=kc: e.matmul(banks[bk][:, 0:nq], lhsT=W[:, kc, wc0:wc0 + 128],
                                                  rhs=XNT[:, kc, c0:c0 + nq], start=(kc == 0), stop=(kc == 7)),
                 reads=XNT_ALL + [wkey], writes=[pk(bk)])

    P.barrier()
    treset()
    load_cache_head(0)
    qT_t = talloc("qT_t", 2048, BF).rearrange("p (h t) -> p h t", h=4)
    szaT = talloc("szaT", 2048, BF).rearrange("p (h t) -> p h t", h=4)
    sq = [talloc("sq%d" % i, 512, F32) for i in range(4)]
    ss8 = [talloc("ss8%d" % i, 8, F32) for i in range(4)]
    rs8 = [talloc("rs8%d" % i, 8, F32) for i in range(4)]
    qnb = [talloc("qnb%d" % i, 512, BF) for i in range(2)]
    knf = [talloc("knf0", 512, F32)] * 2
    kob = [talloc("kob%d" % i, 512, F32) for i in range(2)]
    vfb = [talloc("vfb%d" % i, 512, F32) for i in range(2)]
    thb = [talloc("thb0", 512, F32)] * 2
    ptile2 = [talloc("pt%d" % i, 1024, BF) for i in range(3)]
    dgb = [talloc("dg0", 512, F32)] * 2
    fzz = talloc("fzz", 1024, F32)
    fz0 = fzz[:, 0:512]
    fz1 = fzz[:, 512:1024]
    fto = talloc("fto", 1024, F32)
    ft0 = fto[:, 0:512]
    fo = fto[:, 512:1024]
    fms = fz0
    frs = ft0
    fosq = talloc("fosq", 512, BF)
    nctr = [0]

    def qk_norm(bk, nt, is_k, b, c0loc, tcol0):
        i = nctr[0] % 2
        nctr[0] += 1
        s_, s8, r8 = sq[i], ss8[i], rs8[i]
        P.op("act", lambda e: e.activation(out=s_[0:nt, :], in_=banks[bk][0:nt, :], func=AF.Square),
             reads=[pk(bk)], writes=["sq%d" % i])
        P.op("dve", lambda e: e.tensor_reduce(out=s8[0:nt, :], in_=s_[0:nt, :].rearrange("p (g d) -> p g d", g=8),
                                              axis=AX.X, op=ALU.add), reads=["sq%d" % i], writes=["ss8%d" % i])
        P.op("act", lambda e: e.activation(out=r8[0:nt, :], in_=s8[0:nt, :], func=AF.Ln, bias=epscol[0:nt, 1:2],
                                           scale=1.0), reads=["ss8%d" % i, "epscol"], writes=["rs8a%d" % i])
        P.op("act", lambda e: e.activation(out=r8[0:nt, :], in_=r8[0:nt, :], func=AF.Exp, scale=-0.5),
             reads=["rs8a%d" % i], writes=["rs8%d" % i])
        rb = r8[0:nt, :].unsqueeze(2).to_broadcast([nt, 8, 64])
        src3 = banks[bk][0:nt, :].rearrange("p (g d) -> p g d", g=8)
        if not is_k:
            qn = qnb[i]
            P.op("dve", lambda e: e.tensor_tensor(out=qn[0:nt, :].rearrange("p (g d) -> p g d", g=8), in0=src3,
                                                  in1=rb, op=ALU.mult), reads=[pk(bk), "rs8%d" % i],
                 writes=["qnb%d" % i])
            nkey = "qnb%d" % i
            gcol = qkg[:, 0:1]
        else:
            kf, ko, qn = knf[i], kob[i], qnb[i]
            P.op("dve", lambda e: e.tensor_tensor(out=kf[0:nt, :].rearrange("p (g d) -> p g d", g=8), in0=src3,
                                                  in1=rb, op=ALU.mult), reads=[pk(bk), "rs8%d" % i],
                 writes=["knf0"])
            P.op("dve", lambda e: e.tensor_tensor(out=ko[0:nt, :], in0=kf[0:nt, :],
                                                  in1=gkbc8[0:nt].rearrange("p a b -> p (a b)"), op=ALU.mult),
                 reads=["knf0", "gkbc8"], writes=["kob%d" % i])
            dst = k_s if b == 0 else k_p[(b - 1) * 128:b * 128, :]
            dma("sp", dst, ko[0:nt, :], reads=["kob%d" % i])
            P.op("act", lambda e: e.activation(out=qn[0:nt, :], in_=kf[0:nt, :], func=AF.Copy),
                 reads=["knf0"], writes=["qnb%d" % i])
            nkey = "qnb%d" % i
            gcol = qkg[:, 1:2]
        tb = 6 + trot[0] % 2
        trot[0] += 1
        pst = banks[tb][:, 0:256].bitcast(BF)
        for h in range(4):
            P.op("pe", lambda e, h=h: e.transpose(out=pst[:, h * 128:h * 128 + nt],
                                                  in_=qn[0:nt, h * 128:(h + 1) * 128], identity=ident[0:nt, 0:nt]),
                 reads=[nkey, "ident"], writes=[pk(tb)])
        src = pst.rearrange("p (h t) -> p h t", h=4)[:, :, 0:nt]
        if not is_k:
            P.op("dve", lambda e: e.tensor_scalar(out=qT_t[:, :, c0loc:c0loc + nt], in0=src, scalar1=gcol,
                                                  scalar2=None, op0=ALU.mult), reads=[pk(tb), "qkg"],
                 writes=["qT_t"])
        else:
            if b == 0:
                dstT = kT_s[:, :, 0:nt]
                key = "kT_s"
            else:
                dstT = kT[:, :, tcol0:tcol0 + nt]
                key = "kT%d" % b
            P.op("dve", lambda e: e.tensor_scalar(out=dstT, in0=src, scalar1=gcol, scalar2=None, op0=ALU.mult),
                 reads=[pk(tb), "qkg"], writes=[key])

    def silu_fm(bk, nq, dst, dkey, i):
        th = thb[i % 2]
        P.op("act", lambda e: e.activation(out=th[:, 0:nq], in_=banks[bk][:, 0:nq], func=AF.Tanh, scale=0.5),
             reads=[pk(bk)], writes=["thb0"])
        P.op("dve", lambda e: e.scalar_tensor_tensor(out=dst, in0=th[:, 0:nq], scalar=1.0, in1=banks[bk][:, 0:nq],
                                                     op0=ALU.add, op1=ALU.mult),
             reads=["thb0", pk(bk)], writes=[dkey])

    pctr = [0]

    srotA = [0]

    def attention_head(t, h, nq, kblocks, pending_fin=None):
        O = [4, 5]
        Z = [6, 7]
        nb = len(kblocks)
        sb_ = {}

        def issue_qk(j):
            kb = kblocks[j]
            nk, c0 = kb["nk"], kb["c0"]
            sp = srotA[0] % 2
            srotA[0] += 1
            bs = [2 * sp, 2 * sp + 1]
            pebias = (kb["kind"] == "diag" and nk == 128)
            if kb["kind"] == "grp":
                for gi, mem in enumerate(kb["members"]):
                    for c in range(2):
                        P.op("pe", "matmul", reads=["qT_t"] + mem["rk"], writes=[pk(bs[c])],
                             out=banks[bs[c]][0:128, gi * nq:(gi + 1) * nq], lhsT=mem["kT"][64 * c:64 * c + 64, 0:128],
                             rhs=qT_t[64 * c:64 * c + 64, h, 0:nq], start=True, stop=True)
                sb_[j] = sp
                return
            for c in range(2):
                P.op("pe", "matmul", reads=["qT_t"] + kb["rk"], writes=[pk(bs[c])],
                     out=banks[bs[c]][0:nk, c0:nq], lhsT=kb["kT"][64 * c:64 * c + 64, 0:nk],
                     rhs=qT_t[64 * c:64 * c + 64, h, c0:nq], start=True, stop=(not pebias))
            if pebias:
                for c in range(2):
                    P.op("pe", "matmul", reads=["ident", "Ttab"], writes=[pk(bs[c])],
                         out=banks[bs[c]][0:nk, c0:c0 + 128], lhsT=ident[0:nk, 0:nk], rhs=Ttab[0:nk, h, 0:128],
                         start=False, stop=True)
            sb_[j] = sp

        issue_qk(0)
        if nb > 1:
            issue_qk(1)
        for j in range(nb):
            kb = kblocks[j]
            nk, c0 = kb["nk"], kb["c0"]
            sp = sb_.pop(j)
            bs = [2 * sp, 2 * sp + 1]
            pi = pctr[0] % 3
            pctr[0] += 1
            pt = ptile2[pi]
            bcol = btab[0:nk, kb["bcol"]:kb["bcol"] + 1]
            if kb["kind"] == "grp":
                ng = len(kb["members"])
                sin = psall[0:128, sp * 1024:(sp + 1) * 1024].rearrange("p (c n) -> p c n", c=2)[:, :, 0:ng * nq]
                pout = pt[0:128, :].rearrange("p (c n) -> p c n", c=2)[:, :, 0:ng * nq]
                P.op("act", "activation", reads=[pk(bs[0]), pk(bs[1]), "btab"], writes=["pt%d" % pi],
                     out=pout, in_=sin, func=AF.Exp, bias=bcol, scale=1.0)

                def pv_fn(kb=kb, ng=ng, pt=pt, pi=pi, j=j):
                    for gi, mem in enumerate(kb["members"]):
                        first = (j == 0 and gi == 0)
                        last = (j == nb - 1 and gi == ng - 1)
                        for c in range(2):
                            prhs = pt[0:128, c * 512 + gi * nq:c * 512 + (gi + 1) * nq]
                            P.op("pe", "matmul", reads=["pt%d" % pi] + mem["rv"], writes=[pk(O[c])],
                                 out=banks[O[c]][:, 0:nq], lhsT=mem["v"], rhs=prhs, start=first, stop=last)
                            P.op("pe", "matmul", reads=["pt%d" % pi, "onesw"], writes=[pk(Z[c])],
                                 out=banks[Z[c]][:, 0:nq], lhsT=mem["zl"], rhs=prhs, start=first, stop=last)
            elif kb["kind"] == "diag" and nk < 128:
                for c in range(2):
                    dg = dgb[0]
                    P.op("dve", "tensor_tensor", reads=[pk(bs[c]), "Ttab"], writes=["dg0"],
                         out=dg[0:nk, c0:nq], in0=banks[bs[c]][0:nk, c0:nq], in1=Ttab[0:nk, h, 0:nq - c0],
                         op=ALU.add)
                    P.op("act", "activation", reads=["dg0", "btab"], writes=["pt%d" % pi],
                         out=pt[0:nk, c * 512 + c0:c * 512 + nq], in_=dg[0:nk, c0:nq], func=AF.Exp, bias=bcol,
                         scale=1.0)
            else:
                sin = psall[0:nk, sp * 1024:(sp + 1) * 1024].rearrange("p (c n) -> p c n", c=2)[:, :, c0:nq]
                pout = pt[0:nk, :].rearrange("p (c n) -> p c n", c=2)[:, :, c0:nq]
                P.op("act", "activation", reads=[pk(bs[0]), pk(bs[1]), "btab"], writes=["pt%d" % pi],
                     out=pout, in_=sin, func=AF.Exp, bias=bcol, scale=1.0)
            if kb["kind"] != "grp":
                def pv_fn(kb=kb, nk=nk, c0=c0, pt=pt, pi=pi, j=j):
                    for c in range(2):
                        prhs = pt[0:nk, c * 512 + c0:c * 512 + nq]
                        P.op("pe", "matmul", reads=["pt%d" % pi] + kb["rv"], writes=[pk(O[c])],
                             out=banks[O[c]][:, c0:nq], lhsT=kb["v"], rhs=prhs, start=(j == 0), stop=(j == nb - 1))
                    for c in range(2):
                        prhs = pt[0:nk, c * 512 + c0:c * 512 + nq]
                        P.op("pe", "matmul", reads=["pt%d" % pi, "ones1"], writes=[pk(Z[c])],
                             out=banks[Z[c]][:, c0:nq], lhsT=ones1[0:nk, :], rhs=prhs, start=(j == 0),
                             stop=(j == nb - 1))
            if pending_fin is not None and j == min(9, nb - 1):
                if pending_fin[0] is not None:
                    pending_fin[0]()
                    pending_fin[0] = None
                pending_fin[1](bs[0])
                pending_fin = None
            if j + 2 < nb:
                issue_qk(j + 2)
            pv_fn()
            if pending_fin is not None and j == (0 if nb <= 4 else 1) and pending_fin[0] is not None:
                pending_fin[0]()
                pending_fin[0] = None
        fo_, fosq_, fms_, frs_ = fo, fosq, fz0, ft0
        fok = "fo"
        zin = psall[:, 6 * 512:8 * 512].rearrange("p (c n) -> p c n", c=2)[:, :, 0:nq]
        oin = psall[:, 4 * 512:6 * 512].rearrange("p (c n) -> p c n", c=2)[:, :, 0:nq]
        fz3 = fzz.rearrange("p (c n) -> p c n", c=2)[:, :, 0:nq]
        fo3 = fto.rearrange("p (c n) -> p c n", c=2)[:, :, 0:nq]
        P.op("dve", "tensor_copy", reads=[pk(4), pk(5)], writes=["ft0", fok], out=fo3, in_=oin)
        P.op("dve", "tensor_copy", reads=[pk(6), pk(7)], writes=["fz0", "fz1"], out=fz3, in_=zin)
        tc0 = tile_info(t)[0]

        def fin2a():
            if h == 0:
                P.op("dve", "reciprocal", reads=["fz0", "fz1"], writes=["fz0", "fz1"], out=fz3, in_=fz3)
            elif nb >= 8:
                P.op("dve", "reciprocal", reads=["fz0"], writes=["fz0"], out=fz0[:, 0:nq], in_=fz0[:, 0:nq])
                P.op("act", "activation", reads=["fz1"], writes=["fz1"], out=fz1[:, 0:nq], in_=fz1[:, 0:nq],
                     func=AF.Ln)
                P.op("act", "activation", reads=["fz1"], writes=["fz1"], out=fz1[:, 0:nq], in_=fz1[:, 0:nq],
                     func=AF.Exp, scale=-1.0)
            else:
                P.op("act", "activation", reads=["fz0", "fz1"], writes=["fz0", "fz1"], out=fz3, in_=fz3, func=AF.Ln)
                P.op("act", "activation", reads=["fz0", "fz1"], writes=["fz0", "fz1"], out=fz3, in_=fz3,
                     func=AF.Exp, scale=-1.0)
            P.op("dve", "tensor_tensor", reads=["ft0", "fz0"], writes=["ft0"], out=ft0[:, 0:nq], in0=ft0[:, 0:nq],
                 in1=fz0[:, 0:nq], op=ALU.mult)
            P.op("dve", "tensor_tensor", reads=[fok, "fz1"], writes=["fz1"], out=fz1[:, 0:nq], in0=fo_[:, 0:nq],
                 in1=fz1[:, 0:nq], op=ALU.mult)
            P.op("dve", "scalar_tensor_tensor", reads=["fz1", "ft0", "neglam"], writes=[fok], out=fo_[:, 0:nq],
                 in0=fz1[:, 0:nq], scalar=neglam[:, 0:1], in1=ft0[:, 0:nq], op0=ALU.mult, op1=ALU.add)
            P.op("pool", "tensor_tensor", reads=[fok], writes=["fosq"], out=fosq_[:, 0:nq], in0=fo_[:, 0:nq],
                 in1=fo_[:, 0:nq], op=ALU.mult)

        def fin2b(mb=7):
            P.op("pe", "matmul", reads=["fosq", "ones128"], writes=[pk(mb)], out=banks[mb][:, 0:nq],
                 lhsT=ones128[:], rhs=fosq_[:, 0:nq], start=True, stop=True)
            P.op("act", "activation", reads=[pk(mb), "epscol"], writes=["fz0"], out=fms_[:, 0:nq],
                 in_=banks[mb][:, 0:nq], func=AF.Ln, bias=epscol[:, 0:1], scale=1.0)
            P.op("act", "activation", reads=["fz0"], writes=["ft0"], out=frs_[:, 0:nq], in_=fms_[:, 0:nq],
                 func=AF.Exp, scale=-0.5)
            P.op("dve", "tensor_tensor", reads=[fok, "ft0"], writes=[fok], out=fo_[:, 0:nq], in0=fo_[:, 0:nq],
                 in1=frs_[:, 0:nq], op=ALU.mult)
            P.op("dve", "scalar_tensor_tensor", reads=[fok, "gsub", "szaT"], writes=["oaT%d_%d" % (t, h)],
                 out=oaT[:, h, tc0:tc0 + nq], in0=fo_[:, 0:nq], scalar=gsub[:, 0:1], in1=szaT[:, h, 0:nq],
                 op0=ALU.mult, op1=ALU.mult)
        return [fin2a, fin2b]

    mrot = [0]
    trot = [0]

    def a_stage1(b):
        c0, nt = blk_info(b)
        base = 3 * (mrot[0] % 2)
        mrot[0] += 1
        bq, bk_, bv = base, base + 1, base + 2
        mm_tm(bq, c0, nt, 0, "W0")
        mm_tm(bk_, c0, nt, 512, "W1")
        mm_tm(bv, c0, nt, 1024, "W2")
        return bq, bk_, bv

    blkctr = [0]

    def a_stage2(b, bi, bnk):
        c0, nt = blk_info(b)
        bq, bk_, bv = bnk
        par = blkctr[0] % 2
        blkctr[0] += 1
        iq, ik = 2 * par, 2 * par + 1
        vf = vfb[b % 2]
        ko = kob[b % 2]
        kf = knf[0]
        for bank, i in ((bq, iq), (bk_, ik)):
            P.op("act", "activation", reads=[pk(bank)], writes=["sq%d" % i], out=sq[i][0:nt, :],
                 in_=banks[bank][0:nt, :], func=AF.Square)
        P.op("act", "activation", reads=[pk(bv)], writes=["vfb%d" % (b % 2)], out=vf[0:nt, :],
             in_=banks[bv][0:nt, :], func=AF.Copy)
        dma("sp", v_s if b == 0 else v_p[(b - 1) * 128:b * 128, :], vf[0:nt, :], reads=["vfb%d" % (b % 2)])
        for i in (iq, ik):
            P.op("dve", "tensor_reduce", reads=["sq%d" % i], writes=["ss8%d" % i], out=ss8[i][0:nt, :],
                 in_=sq[i][0:nt, :].rearrange("p (g d) -> p g d", g=8), axis=AX.X, op=ALU.add)
        vdst = v_sbf[0:nt, :] if b == 0 else v_bf[:, b - 1, :]
        vkey = "v_sbf" if b == 0 else "vbf%d" % b
        P.op("dve", "tensor_copy", reads=[pk(bv)], writes=[vkey], out=vdst, in_=banks[bv][0:nt, :])
        for i in (iq, ik):
            P.op("act", "activation", reads=["ss8%d" % i, "epscol"], writes=["rs8a%d" % i], out=rs8[i][0:nt, :],
                 in_=ss8[i][0:nt, :], func=AF.Ln, bias=epscol[0:nt, 1:2], scale=1.0)
            P.op("act", "activation", reads=["rs8a%d" % i], writes=["rs8%d" % i], out=rs8[i][0:nt, :],
                 in_=rs8[i][0:nt, :], func=AF.Exp, scale=-0.5)
        rbq = rs8[iq][0:nt, :].unsqueeze(2).to_broadcast([nt, 8, 64])
        rbk = rs8[ik][0:nt, :].unsqueeze(2).to_broadcast([nt, 8, 64])
        P.op("dve", "tensor_tensor", reads=[pk(bq), "rs8%d" % iq], writes=["qnb0"],
             out=qnb[0][0:nt, :].rearrange("p (g d) -> p g d", g=8),
             in0=banks[bq][0:nt, :].rearrange("p (g d) -> p g d", g=8), in1=rbq, op=ALU.mult)
        P.op("dve", "tensor_tensor", reads=[pk(bk_), "rs8%d" % ik], writes=["knf0"],
             out=kf[0:nt, :].rearrange("p (g d) -> p g d", g=8),
             in0=banks[bk_][0:nt, :].rearrange("p (g d) -> p g d", g=8), in1=rbk, op=ALU.mult)
        pq = banks[6][:, 0:256].bitcast(BF)
        for h in range(4):
            P.op("pe", "transpose", reads=["qnb0", "ident"], writes=[pk(6)], out=pq[:, h * 128:h * 128 + nt],
                 in_=qnb[0][0:nt, h * 128:(h + 1) * 128], identity=ident[0:nt, 0:nt])
        P.op("act", "activation", reads=["knf0"], writes=["qnb1"], out=qnb[1][0:nt, :], in_=kf[0:nt, :],
             func=AF.Copy)
        P.op("dve", "tensor_tensor", reads=["knf0", "gkbc8"], writes=["kob%d" % (b % 2)], out=ko[0:nt, :],
             in0=kf[0:nt, :], in1=gkbc8[0:nt].rearrange("p a b -> p (a b)"), op=ALU.mult)
        dma("sp", k_s if b == 0 else k_p[(b - 1) * 128:b * 128, :], ko[0:nt, :], reads=["kob%d" % (b % 2)])
        pk_ = banks[7][:, 0:256].bitcast(BF)
        for h in range(4):
            P.op("pe", "transpose", reads=["qnb1", "ident"], writes=[pk(7)], out=pk_[:, h * 128:h * 128 + nt],
                 in_=qnb[1][0:nt, h * 128:(h + 1) * 128], identity=ident[0:nt, 0:nt])
        lq = bi * 128
        P.op("dve", "tensor_scalar", reads=[pk(6), "qkg"], writes=["qT_t"], out=qT_t[:, :, lq:lq + nt],
             in0=pq.rearrange("p (h t) -> p h t", h=4)[:, :, 0:nt], scalar1=qkg[:, 0:1], scalar2=None, op0=ALU.mult)
        if b == 0:
            dstT, key = kT_s[:, :, 0:nt], "kT_s"
        else:
            dstT, key = kT[:, :, (b - 1) * 128:(b - 1) * 128 + nt], "kT%d" % b
        P.op("dve", "tensor_scalar", reads=[pk(7), "qkg"], writes=[key], out=dstT,
             in0=pk_.rearrange("p (h t) -> p h t", h=4)[:, :, 0:nt], scalar1=qkg[:, 1:2], scalar2=None, op0=ALU.mult)

    pfin = [None]
    for t in range(5):
        tc0, nq, blks = tile_info(t)
        pend = a_stage1(blks[0])
        if pfin[0] is not None:
            pfin[0][0]()
            pfin[0][1]()
            pfin[0] = None
        for bi, b in enumerate(blks):
            cur = pend
            if bi + 1 < len(blks):
                pend = a_stage1(blks[bi + 1])
            a_stage2(b, bi, cur)
        if t == 0:
            ckpt("A0n")
        for h in range(4):
            bz = gbank()
            mm_fm(bz, tc0, nq, 1536 + 128 * h, "W3")
            silu_fm(bz, nq, szaT[:, h, 0:nq], "szaT", h)
        if t == 0:
            ckpt("A0z")
        if t == 4:
            load_w(2048)
        for h in range(4):
            kbl = []
            if t == 0:
                if h < 3:
                    load_cache_head(h + 1)
                kc_cur = kc_tms[h % 2]
                cv_cur = cvhs[h % 2]
                cvkey = "cvh%d" % (h % 2)
                for g in range(8):
                    tb = gbank()
                    pst = banks[tb][:, 0:256].bitcast(BF)
                    for u in range(4):
                        P.op("pe", "transpose", reads=["kc_tm%d" % (h % 2), "ident"], writes=[pk(tb)],
                             out=pst[:, u * 128:(u + 1) * 128], in_=kc_cur[:, g * 4 + u, :], identity=ident[:])
                    eng = "dve"
                    if eng == "dve":
                        P.op("dve", lambda e, pst=pst, g=g: e.tensor_copy(out=kcT[:, g * 512:(g + 1) * 512], in_=pst),
                             reads=[pk(tb)], writes=["kcT%d" % g])
                    else:
                        P.op("act", lambda e, pst=pst, g=g: e.activation(out=kcT[:, g * 512:(g + 1) * 512], in_=pst,
                                                                         func=AF.Copy), reads=[pk(tb)],
                             writes=["kcT%d" % g])
                cv4 = cv_flat[h % 2].rearrange("p (g r c) -> p g r c", r=4, c=128)
                for d in (1, 2, 3):
                    P.op("dve", "tensor_scalar", reads=[cvkey], writes=[cvkey], out=cv4[:, :, 3 - d, :],
                         in0=cv4[:, :, 3 - d, :], scalar1=float(math.exp(-SLOPES[h] * 128.0 * d)), scalar2=None,
                         op0=ALU.mult)
                for g in range(8):
                    mems = []
                    for j in range(4 * g, 4 * g + 4):
                        d = 4 * g + 3 - j
                        mems.append(dict(kT=kcT[:, j * 128:(j + 1) * 128], v=cv_cur[:, j, :], zl=onesw[:, h, d, :],
                                         rk=["kcT%d" % g], rv=[cvkey]))
                    kbl.append(dict(kind="grp", members=mems, nk=128, c0=0, bcol=BI[("sp", h, 4 * g + 3)]))
                kbl.append(dict(kT=kT_s[:, h, :], v=v_sbf[0:NS, h * 128:(h + 1) * 128], nk=NS, kind="diag",
                                bcol=BI[("sd", h)], c0=0, rk=["kT_s"], rv=["v_sbf"]))
            else:
                for j in range(4 * t):
                    bglob = 1 + j
                    if j < 4 * (t - 1):
                        m = 4 * (t - 1) - j
                        kbl.append(dict(kT=kT[:, h, j * 128:(j + 1) * 128], v=v_bf[:, j, h * 128:(h + 1) * 128],
                                        nk=128, kind="past", bcol=BI[("pp", h, m)], c0=0, rk=["kT%d" % bglob],
                                        rv=["vbf%d" % bglob]))
                    else:
                        m = j - 4 * (t - 1)
                        kbl.append(dict(kT=kT[:, h, j * 128:(j + 1) * 128], v=v_bf[:, j, h * 128:(h + 1) * 128],
                                        nk=128, kind="diag", bcol=BI[("pd", h, m)], c0=128 * m,
                                        rk=["kT%d" % bglob], rv=["vbf%d" % bglob]))
            if t == 0 and h == 0:
                ckpt("A0c")
            pfin[0] = attention_head(t, h, nq, kbl, pfin[0])
            if t == 0 and h == 3:
                pfin[0][0]()
                pfin[0][1]()
                pfin[0] = None
            if t == 0 and h == 0:
                ckpt("A0h")
        if t == 0:
            ckpt("As")
            P.barrier()
        if t == 1:
            ckpt("A1")
    if pfin[0] is not None:
        pfin[0][0]()
        pfin[0][1]()
        pfin[0] = None
    OAT_ALL = ["oaT%d_%d" % (t, h) for t in range(5) for h in range(4)]

    ckpt("A")
    P.barrier()
    treset()
    dma("pool", Woa, w_oa.rearrange("(k p) n -> p k n", p=128), writes=["Woa"])
    dma("pool", Wob, w_ob.rearrange("(k p) n -> p k n", p=128), writes=["Wob"])
    dma("pool", Wout[:, 0:4, :], w_out.rearrange("(k p) n -> p k n", p=128)[:, 0:4, :], writes=["Wout0"])
    dma("pool", Wout[:, 4:8, :], w_out.rearrange("(k p) n -> p k n", p=128)[:, 4:8, :], writes=["Wout1"])
    qbT = talloc("qbT", 2048, BF).rearrange("p (h t) -> p h t", h=4)
    qbxT = talloc("qbxT", 2048, BF).rearrange("p (h t) -> p h t", h=4)
    kbT = talloc("kbT", 2048, BF).rearrange("p (h t) -> p h t", h=4)
    szbT = talloc("szbT", 2048, BF).rearrange("p (h t) -> p h t", h=4)
    kb_tm = talloc("kb_tm", 2048, BF).rearrange("p (b c) -> p b c", c=512)
    vb_tm = talloc("vb_tm", 2048, BF).rearrange("p (b c) -> p b c", c=512)
    zv_tm = talloc("zv_tm", 2048, BF).rearrange("p (b c) -> p b c", c=512)
    thb = [talloc("thbB0", 512, F32)] * 2
    rst = talloc("rst", 4 * 8 * 128, BF).rearrange("p (h c e) -> p h c e", h=4, c=8)
    decbc = talloc("decbc", 512, F32).rearrange("p (h e) -> p h e", h=4)
    sdall = [talloc("sdall%d" % i, 512, BF).rearrange("p (b c) -> p b c", c=128) for i in range(2)]
    gsq = talloc("gsq", 512, BF)
    gms = talloc("gms", 512, F32)
    gto = talloc("gto", 512, F32)
    KSC = 128.0 ** -0.5

    dma("sp", r32[:], st.rearrange("h d e -> d h e"), writes=["r32"])
    for h in range(4):
        P.op("dve", "memset", writes=["decbc"], ap=decbc[:, h, :], constant=DECAY[h])
    sdc = [0]

    for t in range(5):
        tc0, nq, blks = tile_info(t)
        nch = nq // 64
        if t == 1:
            dma("sp", r_s.rearrange("h d e -> d h e"), r32[:], reads=["r32"])
            P.op("dve", "memset", writes=["r32"], ap=r32[:].rearrange("p h e -> p (h e)"), constant=0.0)
        for bi, b in enumerate(blks):
            c0, nt = blk_info(b)
            bk_ = gbank()
            mm_tm(bk_, c0, nt, 512, "W1")
            P.op("act", lambda e, bk_=bk_, bi=bi, nt=nt: e.mul(out=kb_tm[0:nt, bi, :], in_=banks[bk_][0:nt, :], mul=KSC), reads=[pk(bk_)],
                 writes=["kb_tm%d" % bi])
            bv = gbank()
            mm_tm(bv, c0, nt, 1024, "W2")
            P.op("act", lambda e, bv=bv, bi=bi, nt=nt: e.activation(out=vb_tm[0:nt, bi, :], in_=banks[bv][0:nt, :],
                                                                    func=AF.Copy), reads=[pk(bv)],
                 writes=["vb_tm%d" % bi])
            P.op("dve", lambda e, bv=bv, bi=bi, nt=nt: e.tensor_tensor(
                out=zv_tm[0:nt, bi, :].rearrange("p (h c) -> p h c", h=4),
                in0=banks[bv][0:nt, :].rearrange("p (h c) -> p h c", h=4),
                in1=zeta[0:nt, :].unsqueeze(2).to_broadcast([nt, 4, 128]), op=ALU.mult),
                reads=[pk(bv), "zeta"], writes=["zv_tm%d" % bi])
        def fm_head(h):
            bq = gbank()
            mm_fm(bq, tc0, nq, 0 + 128 * h, "W0")
            P.op("act", lambda e, bq=bq, h=h: e.activation(out=qbT[:, h, 0:nq], in_=banks[bq][:, 0:nq], func=AF.Copy),
                 reads=[pk(bq)], writes=["qbT"])
            P.op("dve", lambda e, bq=bq, h=h: e.tensor_tensor(
                out=qbxT[:, h, 0:nq].rearrange("p (n c) -> p n c", c=64),
                in0=banks[bq][:, 0:nq].rearrange("p (n c) -> p n c", c=64),
                in1=xi[:, h, :].unsqueeze(1).to_broadcast([128, nch, 64]), op=ALU.mult),
                reads=[pk(bq), "xi"], writes=["qbxT"])
            bk_ = gbank()
            pstk = banks[bk_][:, 0:256].bitcast(BF)
            for bi, b in enumerate(blks):
                nt_ = blk_info(b)[1]
                P.op("pe", "transpose", reads=["kb_tm%d" % bi, "ident"], writes=[pk(bk_)],
                     out=pstk[:, bi * 128:bi * 128 + nt_], in_=kb_tm[0:nt_, bi, h * 128:(h + 1) * 128],
                     identity=ident[0:nt_, 0:nt_])
            P.op("act", "activation", reads=[pk(bk_)], writes=["kbT"], out=kbT[:, h, 0:nq], in_=pstk[:, 0:nq],
                 func=AF.Copy)
            bz = gbank()
            mm_fm(bz, tc0, nq, 1536 + 128 * h, "W3")
            silu_fm(bz, nq, szbT[:, h, 0:nq], "szbT", h)
        srot = [0]

        def stepA_chunk(ci):
            bi, cc = ci // 2, ci % 2
            ub = 4 + ci % 4
            for h in range(4):
                P.op("pe", "matmul", reads=["kb_tm%d" % bi, "zv_tm%d" % bi], writes=[pk(ub)],
                     out=banks[ub][:, h * 128:(h + 1) * 128],
                     lhsT=kb_tm[64 * cc:64 * cc + 64, bi, h * 128:(h + 1) * 128],
                     rhs=zv_tm[64 * cc:64 * cc + 64, bi, h * 128:(h + 1) * 128], start=True, stop=True)

        def stepB_chunk(ci):
            r32f = r32[:].rearrange("p h e -> p (h e)")
            ub = 4 + ci % 4
            P.op("dve", "tensor_tensor", reads=["r32", "decbc"], writes=["r32"], out=r32f, in0=r32f,
                 in1=decbc[:].rearrange("p h e -> p (h e)"), op=ALU.mult)
            P.op("dve", "tensor_tensor", reads=["r32", pk(ub)], writes=["r32"], out=r32f, in0=r32f,
                 in1=banks[ub][:, 0:512], op=ALU.add)
            if ci + 1 < nch:
                P.op("dve", "tensor_copy", reads=["r32"], writes=["rst_%d" % (ci + 1)], out=rst[:, :, ci + 1, :],
                     in_=r32[:])

        def stepC1(h):
            ob = 2 + (h % 2)
            sbk = h % 2
            nb_ = len(blks)
            nt = blk_info(blks[0])[1]
            for bi in range(nb_):
                lc = bi * 128
                P.op("pe", "matmul", reads=["kbT", "qbT"], writes=[pk(sbk)],
                     out=banks[sbk][0:nt, lc:lc + nt], lhsT=kbT[:, h, lc:lc + nt], rhs=qbT[:, h, lc:lc + nt],
                     start=True, stop=True)
            sd = sdall[h % 2]
            P.op("dve", "tensor_tensor", reads=[pk(sbk), "Dtab"], writes=["sdall%d" % (h % 2)],
                 out=sd[0:nt, 0:nb_, 0:nt],
                 in0=banks[sbk][0:nt, :].rearrange("p (b c) -> p b c", c=128)[:, 0:nb_, 0:nt],
                 in1=Dtab[0:nt, h, 0:nt].unsqueeze(1).to_broadcast([nt, nb_, nt]), op=ALU.mult)

        def stepC1b(h):
            ob = 2 + (h % 2)
            nb_ = len(blks)
            nt = blk_info(blks[0])[1]
            sd = sdall[h % 2]
            for bi in range(nb_):
                lc = bi * 128
                P.op("pe", "matmul", reads=["sdall%d" % (h % 2), "vb_tm%d" % bi], writes=[pk(ob)],
                     out=banks[ob][:, lc:lc + nt], lhsT=vb_tm[0:nt, bi, h * 128:(h + 1) * 128], rhs=sd[0:nt, bi, 0:nt],
                     start=True, stop=False)
                for cc in range(nt // 64):
                    ci = 2 * bi + cc
                    P.op("pe", "matmul", reads=["rst_%d" % ci, "qbxT"], writes=[pk(ob)],
                         out=banks[ob][:, lc + 64 * cc:lc + 64 * cc + 64], lhsT=rst[:, h, ci, :],
                         rhs=qbxT[:, h, lc + 64 * cc:lc + 64 * cc + 64], start=False, stop=(cc == nt // 64 - 1))

        def stepC2(h):
            ob = 2 + (h % 2)
            mb = 4 + h
            P.op("act", "activation", reads=[pk(ob)], writes=["gsq"], out=gsq[:, 0:nq], in_=banks[ob][:, 0:nq],
                 func=AF.Square)
            P.op("pe", "matmul", reads=["gsq", "ones128"], writes=[pk(mb)], out=banks[mb][:, 0:nq], lhsT=ones128[:],
                 rhs=gsq[:, 0:nq], start=True, stop=True)
            P.op("act", "activation", reads=[pk(mb), "epscol"], writes=["gms"], out=gms[:, 0:nq],
                 in_=banks[mb][:, 0:nq], func=AF.Ln, bias=epscol[:, 0:1], scale=1.0)
            P.op("act", "activation", reads=["gms"], writes=["gms"], out=gms[:, 0:nq], in_=gms[:, 0:nq], func=AF.Exp,
                 scale=-0.5)
            P.op("dve", "tensor_tensor", reads=[pk(ob), "gms"], writes=["gto"], out=gto[:, 0:nq],
                 in0=banks[ob][:, 0:nq], in1=gms[:, 0:nq], op=ALU.mult)
            P.op("dve", "scalar_tensor_tensor", reads=["gto", "szbT"], writes=["obT%d_%d" % (t, h)],
                 out=obT[:, h, tc0:tc0 + nq], in0=gto[:, 0:nq], scalar=0.5, in1=szbT[:, h, 0:nq], op0=ALU.mult,
                 op1=ALU.mult)

        for ci in range(min(4, nch)):
            stepA_chunk(ci)
        P.op("dve", "tensor_copy", reads=["r32"], writes=["rst_0"], out=rst[:, :, 0, :], in_=r32[:])
        for h in range(4):
            fm_head(h)
            if t == 4 and h == 3:
                load_w(4096)
            if h >= 1:
                for ci in (2 * (h - 1) + 4, 2 * (h - 1) + 5):
                    if ci < nch:
                        stepA_chunk(ci)
            for ci in (2 * h, 2 * h + 1):
                if ci < nch:
                    stepB_chunk(ci)
        stepC1(0)
        stepC1(1)
        stepC1b(0)
        stepC1(2)
        stepC1b(1)
        stepC2(0)
        stepC1(3)
        stepC1b(2)
        stepC2(1)
        stepC1b(3)
        stepC2(2)
        stepC2(3)
    dma("sp", r_p.rearrange("h d e -> d h e"), r32[:], reads=["r32"])
    OBT_ALL = ["obT%d_%d" % (t, h) for t in range(5) for h in range(4)]

    ckpt("B")
    P.barrier()
    treset()
    mT = talloc("mT", 4096, BF).rearrange("p (k t) -> p k t", k=8)
    tga = [talloc("tga%d" % i, 512, F32) for i in range(2)]
    tgb = [talloc("tgb%d" % i, 512, F32) for i in range(2)]
    u1 = [talloc("u1%d" % i, 512, F32) for i in range(2)]
    u2 = [talloc("u2%d" % i, 512, F32) for i in range(2)]
    xres = [talloc("xres%d" % i, 1024, F32) for i in range(2)]
    yb = [talloc("yb%d" % i, 1024, F32) for i in range(2)]
    allb = [0]

    def abank():
        b = allb[0] % 8
        allb[0] += 1
        return b

    for t in range(5):
        tc0, nq, blks = tile_info(t)
        for fb in range(8):
            i = fb % 2
            b1 = abank()
            mm_fm(b1, tc0, nq, 128 * fb, "W%d" % (fb // 4))
            P.op("act", lambda e, b1=b1, fb=fb, i=i: e.activation(out=tga[i][:, 0:nq], in_=banks[b1][:, 0:nq],
                                                                  func=AF.Tanh, bias=bgh[:, fb:fb + 1], scale=0.5),
                 reads=[pk(b1), "bgh"], writes=["tga%d" % i])
            b2 = abank()
            mm_fm(b2, tc0, nq, 1024 + 128 * fb, "W%d" % (2 + fb // 4))
            P.op("act", lambda e, b2=b2, fb=fb, i=i: e.activation(out=tgb[i][:, 0:nq], in_=banks[b2][:, 0:nq],
                                                                  func=AF.Tanh, bias=bgh[:, 8 + fb:9 + fb], scale=0.5),
                 reads=[pk(b2), "bgh"], writes=["tgb%d" % i])
            b3 = abank()
            for kc in range(4):
                P.op("pe", lambda e, b3=b3, kc=kc, fb=fb: e.matmul(
                    banks[b3][:, 0:nq], lhsT=Woa[:, kc, fb * 128:(fb + 1) * 128], rhs=oaT[:, kc, tc0:tc0 + nq],
                    start=(kc == 0), stop=(kc == 3)), reads=["Woa"] + OAT_ALL, writes=[pk(b3)])
            b4 = abank()
            for kc in range(4):
                P.op("pe", lambda e, b4=b4, kc=kc, fb=fb: e.matmul(
                    banks[b4][:, 0:nq], lhsT=Wob[:, kc, fb * 128:(fb + 1) * 128], rhs=obT[:, kc, tc0:tc0 + nq],
                    start=(kc == 0), stop=(kc == 3)), reads=["Wob"] + OBT_ALL, writes=[pk(b4)])
            P.op("dve", lambda e, b3=b3, i=i: e.scalar_tensor_tensor(
                out=u1[i][:, 0:nq], in0=tga[i][:, 0:nq], scalar=1.0, in1=banks[b3][:, 0:nq], op0=ALU.add,
                op1=ALU.mult), reads=["tga%d" % i, pk(b3)], writes=["u1%d" % i])
            P.op("dve", lambda e, b4=b4, i=i: e.scalar_tensor_tensor(
                out=u2[i][:, 0:nq], in0=tgb[i][:, 0:nq], scalar=1.0, in1=banks[b4][:, 0:nq], op0=ALU.add,
                op1=ALU.mult), reads=["tgb%d" % i, pk(b4)], writes=["u2%d" % i])
            P.op("pool", lambda e, i=i, fb=fb: e.tensor_tensor(out=mT[:, fb, 0:nq], in0=u1[i][:, 0:nq],
                                                               in1=u2[i][:, 0:nq], op=ALU.add),
                 reads=["u1%d" % i, "u2%d" % i], writes=["mT%d" % fb])
        for bi, b in enumerate(blks):
            c0, nt = blk_info(b)
            i = b % 2
            src = x_s if b == 0 else x_p[(b - 1) * 128:b * 128, :]
            dma("sp", xres[i][0:nt, :], src, writes=["xres%d" % i])
            for half in range(2):
                by = abank()
                for kc in range(8):
                    P.op("pe", lambda e, by=by, kc=kc, bi=bi, nt=nt, half=half: e.matmul(
                        banks[by][0:nt, :], lhsT=mT[:, kc, bi * 128:bi * 128 + nt],
                        rhs=Wout[:, kc, half * 512:(half + 1) * 512], start=(kc == 0), stop=(kc == 7)),
                        reads=["mT%d" % kc, "Wout%d" % (kc // 4)], writes=[pk(by)])
                P.op("dve", lambda e, by=by, i=i, nt=nt, half=half: e.scalar_tensor_tensor(
                    out=yb[i][0:nt, half * 512:(half + 1) * 512], in0=banks[by][0:nt, :], scalar=0.5,
                    in1=xres[i][0:nt, half * 512:(half + 1) * 512], op0=ALU.mult, op1=ALU.add),
                    reads=[pk(by), "xres%d" % i], writes=["yb%d_%d" % (i, half)])
            dst = y_s if b == 0 else y_p[(b - 1) * 128:b * 128, :]
            dma("sp", dst, yb[i][0:nt, :], reads=["yb%d_0" % i, "yb%d_1" % i])
    P.emit()
    return nc, P, bias_tab_np


_CACHE = {}


def _get_program():
    if "p" not in _CACHE:
        _CACHE["p"] = build_program()
    return _CACHE["p"]


def kernel(x_prompt, x_sample, cache_k_diff, cache_v_diff, state_ret, norm_g, w_in, b_gate, qn_g, kn_g,
           lam_q1, lam_k1, lam_q2, lam_k2, subln_g, w_o_diff, w_o_ret, w_out):
    f = lambda a: np.ascontiguousarray(np.asarray(a, dtype=np.float32))
    nc, P, bias_tab = _get_program()
    T, Dm, zeta, xi = _const_tables()
    qn = f(qn_g).reshape(64)
    kn = f(kn_g).reshape(64)
    shared = {
        "w_in": f(w_in)[0], "w_oa": f(w_o_diff)[0], "w_ob": f(w_o_ret)[0], "w_out": f(w_out)[0],
        "norm_g": f(norm_g).reshape(D),
        "bgT": np.ascontiguousarray(f(b_gate).reshape(16, 128).T),
        "qk_g": np.ascontiguousarray(np.stack([np.tile(qn, 2), np.tile(kn, 2)], axis=1)),
        "kn_g": kn,
        "lamv": np.ascontiguousarray(np.stack([f(lam_q1).reshape(64), f(lam_k1).reshape(64),
                                               f(lam_q2).reshape(64), f(lam_k2).reshape(64)])),
        "subg": f(subln_g).reshape(128, 1),
        "c_ident": np.eye(128, dtype=np.float32),
        "c_T": T.reshape(128, -1), "c_D": Dm.reshape(128, -1), "c_zeta": zeta, "c_xi": xi.reshape(128, -1),
        "c_bias": bias_tab,
    }
    xp, xs = f(x_prompt), f(x_sample)
    ckd, cvd, srt = f(cache_k_diff), f(cache_v_diff), f(state_ret)
    in_maps = []
    for b in range(8):
        m = dict(shared)
        m["x_p"] = xp[b]
        m["x_s"] = xs[b]
        m["ck"] = ckd[0, b].reshape(PAST, 512)
        m["cv"] = cvd[0, b].reshape(PAST, 512)
        m["st"] = srt[0, b]
        in_maps.append(m)
    res = run_bass_kernel_spmd(nc, in_maps, core_ids=list(range(8)))
    R = res.results
    y_prompt = np.stack([R[b]["y_p"] for b in range(8)])
    y_sample = np.stack([R[b]["y_s"] for b in range(8)])
    k_prompt = np.stack([R[b]["k_p"] for b in range(8)]).reshape(1, 8, SEQ, 4, 2, 64)
    v_prompt = np.stack([R[b]["v_p"] for b in range(8)]).reshape(1, 8, SEQ, 4, 128)
    ret_prompt = np.stack([R[b]["r_p"] for b in range(8)]).reshape(1, 8, 4, 128, 128)
    k_sample = np.stack([R[b]["k_s"] for b in range(8)]).reshape(1, 8, NS, 4, 2, 64)
    v_sample = np.stack([R[b]["v_s"] for b in range(8)]).reshape(1, 8, NS, 4, 128)
    ret_sample = np.stack([R[b]["r_s"] for b in range(8)]).reshape(1, 8, 4, 128, 128)
    return (y_prompt.astype(np.float32), y_sample.astype(np.float32), k_prompt.astype(np.float32),
            v_prompt.astype(np.float32), ret_prompt.astype(np.float32), k_sample.astype(np.float32),
            v_sample.astype(np.float32), ret_sample.astype(np.float32))
```
